# Optimizing a Trainium2 kernel written in Bass

```python
import math
import jax, jax.numpy as jnp
from jax import lax
import numpy as np

D_MODEL = 2048
BATCH = 4
SEQ = 2048
DEPTH = 1

DA_HEADS = 8
DA_HEAD_DIM = 64
D_ATTN = DA_HEADS * 2 * DA_HEAD_DIM
ROPE_THETA = 500000.0
ROT_DIM = DA_HEAD_DIM // 4
Q_BLOCK = 128
D_CONV = 1024
CONV_WIDTH = 31
NORM_EPS = 1e-6
SUBLN_EPS = 1e-5
LN_EPS = 1e-5
SPLITS = (D_ATTN, D_ATTN, D_ATTN, D_ATTN, 2 * D_CONV, D_CONV, D_MODEL, D_MODEL)
D_IN = D_ATTN * 4 + 2 * D_CONV + D_CONV + 2 * D_MODEL

kernel_name = 'hybrid_diffattn_conformer_gated_merge'


def lambda_init_fn(layer):
    return 0.8 - 0.6 * math.exp(-0.3 * layer)


def rms_norm(x, w, eps):
    xf = x.astype(jnp.float32)
    y = xf * lax.rsqrt(jnp.mean(xf * xf, axis=-1, keepdims=True) + eps)
    return (y * w.astype(jnp.float32)).astype(x.dtype)


def layer_norm(x, g, b, eps):
    xf = x.astype(jnp.float32)
    mu = jnp.mean(xf, axis=-1, keepdims=True)
    xc = xf - mu
    var = jnp.mean(xc * xc, axis=-1, keepdims=True)
    y = xc * lax.rsqrt(var + eps) * g.astype(jnp.float32) + b.astype(jnp.float32)
    return y.astype(x.dtype)


def partial_rotary(x, positions):
    half = ROT_DIM // 2
    inv_freq = ROPE_THETA ** (-jnp.arange(0, ROT_DIM, 2, dtype=jnp.float32) / ROT_DIM)
    ang = positions.astype(jnp.float32)[:, None] * inv_freq[None, :]
    cos = jnp.cos(ang)[None, :, None, None, :]
    sin = jnp.sin(ang)[None, :, None, None, :]
    xf = x.astype(jnp.float32)
    x1 = xf[..., :half]
    x2 = xf[..., half:ROT_DIM]
    r1 = (x1 * cos - x2 * sin).astype(x.dtype)
    r2 = (x2 * cos + x1 * sin).astype(x.dtype)
    return jnp.concatenate([r1, r2, x[..., ROT_DIM:]], axis=-1)


def diff_attention(q, k, v, lam):
    B, S, H, _, d = q.shape
    nb = S // Q_BLOCK
    scale = d ** -0.5
    qb = q.reshape(B, nb, Q_BLOCK, H, 2, d).transpose(1, 0, 2, 3, 4, 5)
    kpos = jnp.arange(S)
    neg = jnp.finfo(jnp.float32).min

    def block(args):
        qblk, i = args
        s = jnp.einsum('bqhcd,bkhcd->bhcqk', qblk, k,
                       preferred_element_type=jnp.float32) * scale
        qpos = i * Q_BLOCK + jnp.arange(Q_BLOCK)
        mask = qpos[:, None] >= kpos[None, :]
        s = jnp.where(mask, s, neg)
        p = jax.nn.softmax(s, axis=-1)
        a = p[:, :, 0] - lam * p[:, :, 1]
        return jnp.einsum('bhqk,bkhe->bqhe', a.astype(v.dtype), v)

    o = lax.map(block, (qb, jnp.arange(nb)))
    return o.transpose(1, 0, 2, 3, 4).reshape(B, S, H, 2 * d)


def causal_depthwise_conv(u, w, b):
    c = u.shape[-1]
    y = lax.conv_general_dilated(
        u, w[:, None, :].astype(u.dtype), window_strides=(1,),
        padding=[(CONV_WIDTH - 1, 0)], dimension_numbers=('NWC', 'WIO', 'NWC'),
        feature_group_count=c)
    return y + b.astype(u.dtype)


def setup_inputs(seed: int = 0) -> dict:
    key = jax.random.key(seed)
    ks = jax.random.split(key, 20)
    f32 = jnp.float32
    L = DEPTH

    def nrm(k, shape, scale):
        return jax.random.normal(k, shape, f32) * scale

    return {
        'x': jax.random.normal(ks[0], (BATCH, SEQ, D_MODEL), f32),
        'norm_pre_w': 1.0 + nrm(ks[1], (L, D_MODEL), 0.01),
        'w_in': nrm(ks[2], (L, D_MODEL, D_IN), D_MODEL ** -0.5),
        'lambda_q1': nrm(ks[3], (L, DA_HEAD_DIM), 0.1),
        'lambda_k1': nrm(ks[4], (L, DA_HEAD_DIM), 0.1),
        'lambda_q2': nrm(ks[5], (L, DA_HEAD_DIM), 0.1),
        'lambda_k2': nrm(ks[6], (L, DA_HEAD_DIM), 0.1),
        'subln_w': 1.0 + nrm(ks[7], (L, 2 * DA_HEAD_DIM), 0.01),
        'w_o_attn': nrm(ks[8], (L, D_ATTN, D_MODEL), D_ATTN ** -0.5),
        'b_glu': nrm(ks[9], (L, 2 * D_CONV), 0.01),
        'w_dw': nrm(ks[10], (L, CONV_WIDTH, D_CONV), CONV_WIDTH ** -0.5),
        'b_dw': nrm(ks[11], (L, D_CONV), 0.01),
        'ln_g': 1.0 + nrm(ks[12], (L, D_CONV), 0.01),
        'ln_b': nrm(ks[13], (L, D_CONV), 0.01),
        'w_pw2': nrm(ks[14], (L, D_CONV, D_MODEL), D_CONV ** -0.5),
        'b_pw2': nrm(ks[15], (L, D_MODEL), 0.01),
        'w_out': nrm(ks[16], (L, D_MODEL, D_MODEL), D_MODEL ** -0.5),
        'norm_post_w': 1.0 + nrm(ks[17], (L, D_MODEL), 0.01),
    }


def reference(x, norm_pre_w, w_in, lambda_q1, lambda_k1, lambda_q2, lambda_k2, subln_w,
              w_o_attn, b_glu, w_dw, b_dw, ln_g, ln_b, w_pw2, b_pw2, w_out, norm_post_w):
    B, S, _ = x.shape
    positions = jnp.arange(S)
    split_points = [int(p) for p in np.cumsum(SPLITS)[:-1]]
    for l in range(DEPTH):
        h = rms_norm(x, norm_pre_w[l], NORM_EPS)
        proj = h @ w_in[l]
        q, k, v, z_a, u_glu, z_c, ga, gc = jnp.split(proj, split_points, axis=-1)

        q = partial_rotary(q.reshape(B, S, DA_HEADS, 2, DA_HEAD_DIM), positions)
        k = partial_rotary(k.reshape(B, S, DA_HEADS, 2, DA_HEAD_DIM), positions)
        v = v.reshape(B, S, DA_HEADS, 2 * DA_HEAD_DIM)
        lam_init = lambda_init_fn(l)
        lam = (jnp.exp(jnp.sum(lambda_q1[l].astype(jnp.float32) * lambda_k1[l].astype(jnp.float32)))
               - jnp.exp(jnp.sum(lambda_q2[l].astype(jnp.float32) * lambda_k2[l].astype(jnp.float32)))
               + lam_init)
        o = diff_attention(q, k, v, lam)
        o = rms_norm(o, subln_w[l], SUBLN_EPS) * (1.0 - lam_init)
        o = o.reshape(B, S, D_ATTN) * jax.nn.silu(z_a)
        y_attn = o @ w_o_attn[l]

        u = u_glu + b_glu[l]
        u = u[..., :D_CONV] * jax.nn.sigmoid(u[..., D_CONV:])
        u = causal_depthwise_conv(u, w_dw[l], b_dw[l])
        u = layer_norm(u, ln_g[l], ln_b[l], LN_EPS)
        u = jax.nn.silu(u) * jax.nn.silu(z_c)
        y_conv = u @ w_pw2[l] + b_pw2[l]

        m = jax.nn.sigmoid(ga) * y_attn + jax.nn.sigmoid(gc) * y_conv
        out = m @ w_out[l]
        x = x + rms_norm(out, norm_post_w[l], NORM_EPS)
    return x
```

```python
import math
from contextlib import ExitStack

import numpy as np
import ml_dtypes

import concourse.bass as bass
import concourse.mybir as mybir
from concourse.bass_utils import run_bass_kernel_spmd

F32 = mybir.dt.float32
BF16 = mybir.dt.bfloat16
AF = mybir.ActivationFunctionType
ALU = mybir.AluOpType

D = 2048
S = 2048
NB = 4
T = 1024
DIN = 11264
C_Q, C_K, C_V, C_ZA, C_GLU, C_ZC, C_GA, C_GC = 0, 1024, 2048, 3072, 4096, 6144, 7168, 9216
LAM_INIT = 0.8 - 0.6 * math.exp(-0.3 * 0)
CONVW = 31
ROPE_THETA = 500000.0

CF_COS, CF_SIN, CF_BGLU, CF_BDW, CF_LNG, CF_LNB, CF_BPW2, CF_WDW, CF_LAM, CF_SUBLN, CF_HALO, CF_N = \
    0, 128, 256, 272, 280, 288, 296, 312, 560, 816, 944, 945
CB_ID, CB_TRI, CB_VONES, CB_N = 0, 128, 256, 272


class Trk:
    ENG = ('pe', 'act', 'dve', 'pool', 'sp')

    def __init__(self, nc, es):
        self.nc = nc
        self.es = es
        self.sem = {k: es.enter_context(nc.semaphore("s_" + k)) for k in self.ENG}
        self.cnt = {k: 0 for k in self.ENG}
        self.seen = {k: {} for k in self.ENG}
        self.prog = {k: [] for k in self.ENG}
        self.lastw = {}
        self.readers = {}
        self.dsem = {}
        self.dcnt = {}
        self.pe_dirty = False

    def _wait(self, e, key, val):
        if self.seen[e].get(key, 0) >= val:
            return
        if key == e and val > self.cnt[e]:
            return
        self.seen[e][key] = val
        self.prog[e].append(('wait', key, val))

    def _deps(self, e, reads, writes):
        best = {}
        for r in reads:
            tk = self.lastw.get(r)
            if tk is not None:
                best[tk[0]] = max(best.get(tk[0], 0), tk[1])
        for w in writes:
            tk = self.lastw.get(w)
            if tk is not None:
                best[tk[0]] = max(best.get(tk[0], 0), tk[1])
            for k, v in self.readers.get(w, ()):
                best[k] = max(best.get(k, 0), v)
        for k, v in best.items():
            self._wait(e, k, v)

    def _record(self, tok, reads, writes):
        for r in reads:
            lst = self.readers.setdefault(r, [])
            for i, (k, v) in enumerate(lst):
                if k == tok[0]:
                    lst[i] = (k, max(v, tok[1]))
                    break
            else:
                lst.append(tok)
        for w in writes:
            self.lastw[w] = tok
            self.readers[w] = []

    def op(self, e, fn, reads=(), writes=(), sig=True):
        self._deps(e, reads, writes)
        if sig:
            self.cnt[e] += 1
            tok = (e, self.cnt[e])
            if e == 'pe':
                self.pe_dirty = False
        else:
            tok = (e, self.cnt[e] + 1)
            if e == 'pe':
                self.pe_dirty = True
        self.prog[e].append(('op', fn, sig))
        self._record(tok, reads, writes)
        return tok

    def dma(self, e, out, in_, reads=(), writes=(), key=None):
        assert key is not None
        if key not in self.dsem:
            self.dsem[key] = self.es.enter_context(self.nc.semaphore("d%d" % len(self.dsem)))
            self.dcnt[key] = 0
        self._deps(e, reads, writes)
        self.dcnt[key] += 16
        self.prog[e].append(('dma', (out, in_), key))
        tok = (key, self.dcnt[key])
        self._record(tok, reads, writes)
        return tok

    def barrier(self, engines=('pe', 'act', 'dve', 'pool', 'sp')):
        assert not self.pe_dirty
        for e in engines:
            for o in engines:
                if o != e and self.cnt[o] > 0:
                    self._wait(e, o, self.cnt[o])
            for k in list(self.dcnt):
                if isinstance(k, tuple) and k[0] == 'dbg':
                    self._wait(e, k, self.dcnt[k])

    def wait_dbg(self, engines=('pe', 'act', 'dve', 'pool', 'sp')):
        for e in engines:
            for k in list(self.dcnt):
                if isinstance(k, tuple) and k[0] == 'dbg':
                    self._wait(e, k, self.dcnt[k])

    def wait_all_dma(self, e, keys):
        for k in keys:
            self._wait(e, k, self.dcnt[k])

    def check(self):
        val = {}
        ptr = {e: 0 for e in self.ENG}
        progress = True
        while progress:
            progress = False
            for e in self.ENG:
                p = self.prog[e]
                while ptr[e] < len(p):
                    it = p[ptr[e]]
                    if it[0] == 'wait':
                        if val.get(it[1], 0) < it[2]:
                            break
                    elif it[0] == 'op':
                        if it[2]:
                            val[e] = val.get(e, 0) + 1
                    else:
                        val[it[2]] = val.get(it[2], 0) + 16
                    ptr[e] += 1
                    progress = True
        for e in self.ENG:
            if ptr[e] < len(self.prog[e]):
                raise RuntimeError("DEADLOCK: engine %s stuck at %d/%d: %r" % (
                    e, ptr[e], len(self.prog[e]), self.prog[e][ptr[e]][:3]))

    def emit(self):
        self.check()
        nc = self.nc

        def run(e, eng):
            sem = self.sem[e]
            for it in self.prog[e]:
                if it[0] == 'wait':
                    k = it[1]
                    eng.wait_ge(self.sem[k] if k in self.sem else self.dsem[k], it[2])
                elif it[0] == 'op':
                    ins = it[1](eng)
                    if it[2]:
                        ins.then_inc(sem, 1)
                else:
                    eng.dma_start(out=it[1][0], in_=it[1][1]).then_inc(self.dsem[it[2]], 16)

        with nc.Block() as block:
            @block.tensor
            def _(eng):
                run('pe', eng)

            @block.scalar
            def _(eng):
                run('act', eng)

            @block.vector
            def _(eng):
                run('dve', eng)

            @block.gpsimd
            def _(eng):
                run('pool', eng)

            @block.sync
            def _(eng):
                run('sp', eng)


class Bump:
    def __init__(self, ap, ncols):
        self.ap = ap
        self.n = ncols
        self.off = 0

    def reset(self):
        self.off = 0

    def bf(self, n):
        assert self.off + n <= self.n, (self.off, n, self.n)
        a = self.ap[:, self.off:self.off + n]
        self.off += n
        return a

    def f32(self, n):
        if self.off % 2:
            self.off += 1
        return self.bf(2 * n).bitcast(F32)


WARM_JUNK = 0
FIN_DELAY = 2
ARENA_COLS = 37376


def build(debug=False, stop_after=None):
    nc = bass.Bass("TRN2", target_bir_lowering=False)

    def dram(name, shape, dt, kind="ExternalInput"):
        return nc.dram_tensor(name, shape, dt, kind=kind).ap()

    x_all = dram("x_all", [2048, D], F32)
    w_in = dram("w_in", [D, DIN], F32)
    w_o = dram("w_o", [1024, D], F32)
    w_pw2 = dram("w_pw2", [1024, D], F32)
    w_out = dram("w_out", [D, D], F32)
    cf_d = dram("cf", [128, CF_N], F32)
    cb_d = dram("cb", [128, CB_N], BF16)
    wvec_d = dram("wvec", [128, 2 * D], F32)
    out_d = dram("out", [T, D], F32, kind="ExternalOutput")
    dbg = {}
    if debug:
        for nm, shp, dt in (("d_hT", [128, 16 * 2048], BF16), ("d_uT", [128, 8 * 1056], BF16),
                            ("d_u2T", [128, 8 * 1024], BF16), ("d_ozT", [128, 8 * 1024], BF16),
                            ("d_mT", [128, 16 * 1024], BF16), ("d_zcT", [128, 8 * 1024], BF16),
                            ("d_KT", [128, 4 * 2048], BF16), ("d_Q0", [128, 4 * 1024], BF16),
                            ("d_vaug", [128, 16 * 4 * 130], BF16), ("d_zsb", [128, 8 * 512], BF16)):
            dbg[nm] = dram(nm, shp, dt, kind="ExternalOutput")

    w_in_v = w_in.rearrange("(a p) n -> p a n", p=128)
    w_o_v = w_o.rearrange("(a p) n -> p a n", p=128)
    w_pw2_v = w_pw2.rearrange("(a p) n -> p a n", p=128)
    w_out_v = w_out.rearrange("(a p) n -> p a n", p=128)

    es = ExitStack()
    with es:
        t = Trk(nc, es)
        hT_t = es.enter_context(nc.sbuf_tensor("hT", [128, 16 * 2048], BF16))
        ozT_t = es.enter_context(nc.sbuf_tensor("ozT", [128, 8 * 1024], BF16))
        u2T_t = es.enter_context(nc.sbuf_tensor("u2T", [128, 8 * 1024], BF16))
        wr_t = es.enter_context(nc.sbuf_tensor("wring", [128, 16384], BF16))
        cf_t = es.enter_context(nc.sbuf_tensor("cfs", [128, CF_N + 3], F32))
        cb_t = es.enter_context(nc.sbuf_tensor("cbs", [128, CB_N], BF16))
        sm_t = es.enter_context(nc.sbuf_tensor("small", [128, 640], F32))
        arena_t = es.enter_context(nc.sbuf_tensor("arena", [128, ARENA_COLS], BF16))
        ps_t = es.enter_context(nc.psum_tensor("ps", [128, 4096], F32))

        hT = hT_t[:, :].rearrange("p (a n) -> p a n", a=16)
        ozT = ozT_t[:, :].rearrange("p (a n) -> p a n", a=8)
        u2T = u2T_t[:, :].rearrange("p (a n) -> p a n", a=8)
        wring = wr_t[:, :]
        cfs = cf_t[:, :]
        cbs = cb_t[:, :]
        small = sm_t[:, :]
        ps = ps_t[:, :]
        A = Bump(arena_t[:, :], ARENA_COLS)

        def finish():
            t.barrier()
            t.wait_all_dma('sp', list(t.dsem.keys()))
            print("instr counts:", {e: len(t.prog[e]) for e in t.ENG}, "sems:", len(t.dsem) + 5)
            t.emit()
            return nc

        ident = cbs[:, CB_ID:CB_ID + 128]
        tri = cbs[:, CB_TRI:CB_TRI + 128]
        vones = cbs[:, CB_VONES:CB_VONES + 16]
        cosv = cfs[:, CF_COS:CF_COS + 128].rearrange("p (t f) -> p t f", t=16)
        sinv = cfs[:, CF_SIN:CF_SIN + 128].rearrange("p (t f) -> p t f", t=16)
        bglu = cfs[:, CF_BGLU:CF_BGLU + 16]
        bdw = cfs[:, CF_BDW:CF_BDW + 8]
        lng = cfs[:, CF_LNG:CF_LNG + 8]
        lnb = cfs[:, CF_LNB:CF_LNB + 8]
        bpw2 = cfs[:, CF_BPW2:CF_BPW2 + 16]
        wdw = cfs[:, CF_WDW:CF_WDW + 248].rearrange("p (c j) -> p c j", c=8)
        lamv = cfs[:, CF_LAM:CF_LAM + 256]
        sublnw = cfs[:, CF_SUBLN:CF_SUBLN + 128]
        halo = cfs[:, CF_HALO:CF_HALO + 1]
        eps6 = cfs[:, CF_N:CF_N + 1]
        eps5 = cfs[:, CF_N + 1:CF_N + 2]
        neglam = cfs[:, CF_N + 2:CF_N + 3]

        ssq = small[:, 0:16]
        sqv = small[:, 16:32]
        rstd = small[:, 32:48]
        lamt = small[:, 48:56]
        ssq2 = small[:, 56:64]
        sq2 = small[:, 64:72]
        rstd2 = small[:, 72:80]
        fin = small[:, 128:512].rearrange("p (k s) -> p k s", s=6)

        def bank(b, n=512, off=0):
            return ps[:, 512 * b + off:512 * b + off + n]

        def bank_bf(b):
            return ps[:, 512 * b:512 * b + 512].bitcast(BF16)

        def load_w(col0, ncols, src3, wkey):
            a_tot, n = src3.shape[1], src3.shape[2]
            assert a_tot * n == ncols
            dst = wring[:, col0:col0 + ncols].rearrange("p (a n) -> p a n", a=a_tot)
            step = max(1, a_tot // 4)
            res_all = []
            for a0 in range(0, a_tot, step):
                c_lo = col0 + a0 * n
                c_hi = col0 + (a0 + step) * n
                res = [('W', q) for q in range(c_lo // 1024, (c_hi - 1) // 1024 + 1)]
                tok = t.dma('pool', dst[:, a0:a0 + step, :], src3[:, a0:a0 + step, :], writes=res, key=wkey)
                res_all += res
            for r in res_all:
                t.lastw[r] = tok
            return sorted(set(res_all))

        t.dma('pool', cfs[:, 0:CF_N], cf_d, writes=['cf'], key='cf')
        t.dma('pool', cbs, cb_d, writes=['cb'], key='cb')
        t.op('dve', lambda e: e.memset(small, 0.0), writes=['small'])
        t.op('dve', lambda e: e.memset(eps6, 1e-6), writes=['eps6'])
        t.op('dve', lambda e: e.memset(eps5, 1e-5), writes=['eps5'])
        lj = small[:, 512:640]
        t.op('dve', lambda e: e.tensor_tensor(out=lj[:, 0:64], in0=lamv[:, 0:64], in1=lamv[:, 64:128], op=ALU.mult),
             reads=['cf', 'small'], writes=['lj0'])
        t.op('dve', lambda e: e.tensor_tensor(out=lj[:, 64:128], in0=lamv[:, 128:192], in1=lamv[:, 192:256], op=ALU.mult),
             reads=['cf', 'small'], writes=['lj1'])
        t.op('dve', lambda e: e.tensor_reduce(out=lamt[:, 0:1], in_=lj[:, 0:64], axis=mybir.AxisListType.X, op=ALU.add),
             reads=['lj0', 'small'], writes=['lam0'])
        t.op('dve', lambda e: e.tensor_reduce(out=lamt[:, 1:2], in_=lj[:, 64:128], axis=mybir.AxisListType.X, op=ALU.add),
             reads=['lj1', 'small'], writes=['lam1'])
        t.op('act', lambda e: e.activation(out=lamt[:, 2:4], in_=lamt[:, 0:2], func=AF.Exp), reads=['lam0', 'lam1'], writes=['lam2'])
        t.op('dve', lambda e: e.scalar_tensor_tensor(out=neglam, in0=lamt[:, 3:4], scalar=-LAM_INIT, in1=lamt[:, 2:3],
                                                     op0=ALU.add, op1=ALU.subtract), reads=['lam2'], writes=['neglam'])
        t.op('dve', lambda e: e.tensor_scalar(out=sublnw, in0=sublnw, scalar1=1.0 - LAM_INIT, scalar2=None, op0=ALU.mult),
             reads=['cf'], writes=['sublnw'])

        A.reset()
        xs = [A.f32(D) for _ in range(4)]
        htm = [A.bf(D), A.bf(D)]
        junk = A.bf(D)
        wvsA = A.f32(D)
        t.dma('pool', wvsA, wvec_d[:, 0:D], writes=['wv'], key='wv')
        a_cnt = [0]

        def A_block(tb):
            p = a_cnt[0] % 2
            px = a_cnt[0] % 4
            a_cnt[0] += 1
            t.dma('sp', xs[px], x_all[tb * 128:(tb + 1) * 128, :], writes=[('xs', px)], key=('xs', px))
            t.op('act', lambda e, px=px, tb=tb: e.activation(out=junk, in_=xs[px], func=AF.Square, accum_out=ssq[:, tb:tb + 1]),
                 reads=[('xs', px), 'small'], writes=['junk', ('ssq', tb)])
            t.op('act', lambda e, tb=tb: e.activation(out=sqv[:, tb:tb + 1], in_=ssq[:, tb:tb + 1], func=AF.Sqrt,
                                                      scale=1.0 / D, bias=eps6), reads=[('ssq', tb), 'eps6'], writes=[('sqv', tb)])
            t.op('dve', lambda e, tb=tb: e.reciprocal(out=rstd[:, tb:tb + 1], in_=sqv[:, tb:tb + 1]),
                 reads=[('sqv', tb)], writes=[('rstd', tb)])
            t.op('dve', lambda e, p=p, px=px, tb=tb: e.scalar_tensor_tensor(out=htm[p], in0=xs[px], scalar=rstd[:, tb:tb + 1], in1=wvsA,
                                                                     op0=ALU.mult, op1=ALU.mult),
                 reads=[('xs', px), ('rstd', tb), 'wv'], writes=[('htm', p)])
            return p

        def A_block2(tb, p):
            b0 = 2 * p
            for dc in range(16):
                b = b0 + dc // 8
                o = bank_bf(b)[:, (dc % 8) * 128:(dc % 8) * 128 + 128]
                t.op('pe', lambda e, o=o, p=p, dc=dc: e.transpose(out=o, in_=htm[p][:, dc * 128:(dc + 1) * 128], identity=ident),
                     reads=[('htm', p), 'cb'], writes=[('ps', b)], sig=(dc % 8 == 7))
            src0 = bank_bf(b0).rearrange("p (a n) -> p a n", a=8)
            src1 = bank_bf(b0 + 1).rearrange("p (a n) -> p a n", a=8)
            t.op('act', lambda e, src0=src0, tb=tb: e.activation(out=hT[:, 0:8, tb * 128:(tb + 1) * 128], in_=src0, func=AF.Copy),
                 reads=[('ps', b0)], writes=[('hT', tb, 0)])
            t.op('dve', lambda e, src1=src1, tb=tb: e.tensor_copy(out=hT[:, 8:16, tb * 128:(tb + 1) * 128], in_=src1),
                 reads=[('ps', b0 + 1)], writes=[('hT', tb, 1)])
        hT_all = [('hT', tb, k) for tb in range(16) for k in range(2)]
        hT_own = [('hT', tb, k) for tb in range(8, 16) for k in range(2)]

        if stop_after == 'A':
            return finish()
        def e1_load(cp):
            u = (cp % 2) * 2
            ra = load_w(u * 4096, 4096, w_in_v[:, :, C_GLU + cp * 256:C_GLU + cp * 256 + 256], ('Wk', u))
            rg = load_w((u + 1) * 4096, 4096, w_in_v[:, :, C_GLU + 1024 + cp * 256:C_GLU + 1024 + cp * 256 + 256], ('Wk', u + 1))
            return u, ra, rg
        pend = e1_load(0)
        prevA = None
        for tb in range(16):
            p_ = A_block(tb)
            if prevA is not None:
                A_block2(*prevA)
            prevA = (tb, p_)
        A_block2(*prevA)
        a_rest = []
        t.barrier()
        A.reset()
        uT = A.bf(8 * 1056).rearrange("p (a n) -> p a n", a=8)
        zcT = A.bf(8 * 1024).rearrange("p (a n) -> p a n", a=8)
        e3_off = A.off
        sgs = [A.f32(352), A.f32(352)]
        it = 0
        for cp in range(4):
            u, ra, rg = pend
            if cp + 1 < 4:
                pend = e1_load(cp + 1)
            wa = wring[:, u * 4096:(u + 1) * 4096].rearrange("p (a n) -> p a n", a=16)
            wg = wring[:, (u + 1) * 4096:(u + 2) * 4096].rearrange("p (a n) -> p a n", a=16)
            for ci in range(2):
                cc = cp * 2 + ci
                for g in range(3):
                    c0 = 992 + 352 * g
                    hres = [('hT', tb_, k_) for tb_ in range(c0 // 128, (c0 + 351) // 128 + 1) for k_ in range(2)]
                    par = it % 2
                    it += 1
                    bA, bG = 2 * par, 2 * par + 1
                    for dc in range(16):
                        t.op('pe', lambda e, dc=dc, bA=bA, c0=c0, wa=wa, ci=ci: e.matmul(
                            bank(bA, 352), lhsT=wa[:, dc, ci * 128:(ci + 1) * 128], rhs=hT[:, dc, c0:c0 + 352],
                            start=(dc == 0), stop=(dc == 15)), reads=ra + hres, writes=[('ps', bA)], sig=(dc == 15))
                    for dc in range(16):
                        t.op('pe', lambda e, dc=dc, bG=bG, c0=c0, wg=wg, ci=ci: e.matmul(
                            bank(bG, 352), lhsT=wg[:, dc, ci * 128:(ci + 1) * 128], rhs=hT[:, dc, c0:c0 + 352],
                            start=(dc == 0), stop=(dc == 15)), reads=rg + hres, writes=[('ps', bG)], sig=(dc == 15))
                    t.op('act', lambda e, par=par, bG=bG, cc=cc: e.activation(out=sgs[par], in_=bank(bG, 352), func=AF.Sigmoid,
                                                                              bias=bglu[:, 8 + cc:9 + cc]),
                         reads=[('ps', bG), 'cf'], writes=[('sgs', par)])
                    t.op('dve', lambda e, par=par, bA=bA, cc=cc, g=g: e.scalar_tensor_tensor(
                        out=uT[:, cc, 352 * g:352 * g + 352], in0=bank(bA, 352), scalar=bglu[:, cc:cc + 1], in1=sgs[par],
                        op0=ALU.add, op1=ALU.mult), reads=[('ps', bA), ('sgs', par), 'cf'], writes=[('uT', cc, g)])
                    if a_rest:
                        A_block(a_rest.pop(0))
        uT_all = [('uT', cc, g) for cc in range(8) for g in range(3)]
        t.op('dve', lambda e: e.tensor_scalar(out=uT[:, :, 0:32], in0=uT[:, :, 0:32], scalar1=halo, scalar2=None, op0=ALU.mult),
             reads=[('uT', cc, 0) for cc in range(8)] + ['cf'], writes=[('uT', cc, 0) for cc in range(8)])

        if stop_after == 'E1':
            return finish()
        def e2_load(cq):
            u = (cq % 2) * 2
            r = load_w(u * 4096, 8192, w_in_v[:, :, C_ZC + cq * 512:C_ZC + cq * 512 + 512], ('Wk', u))
            return u, r
        pend = e2_load(0)
        for cq in range(2):
            u, rw = pend
            if cq + 1 < 2:
                pend = e2_load(cq + 1)
            wz = wring[:, u * 4096:(u + 2) * 4096].rearrange("p (a n) -> p a n", a=16)
            for ci in range(4):
                cc = cq * 4 + ci
                for tg in range(2):
                    par = it % 2
                    it += 1
                    b = par
                    for dc in range(16):
                        t.op('pe', lambda e, dc=dc, b=b, wz=wz, ci=ci, tg=tg: e.matmul(
                            bank(b), lhsT=wz[:, dc, ci * 128:(ci + 1) * 128], rhs=hT[:, dc, 1024 + tg * 512:1024 + tg * 512 + 512],
                            start=(dc == 0), stop=(dc == 15)), reads=rw + hT_own, writes=[('ps', b)], sig=(dc == 15))
                    t.op('act', lambda e, b=b, cc=cc, tg=tg: e.activation(out=zcT[:, cc, tg * 512:(tg + 1) * 512], in_=bank(b), func=AF.Silu),
                         reads=[('ps', b)], writes=[('zcT', cc, tg)])

        if stop_after == 'E2':
            return finish()
        assert not a_rest
        A.off = e3_off
        yT = A.f32(8 * 512).rearrange("p (a n) -> p a n", a=8)
        ysq = [A.f32(512), A.f32(512)]
        mean_b = A.f32(512)
        var_b = A.f32(512)
        diag = [A.bf(31 * 128).rearrange("p (j n) -> p j n", j=31), A.bf(31 * 128).rearrange("p (j n) -> p j n", j=31)]
        onesf = A.f32(128)
        t.op('dve', lambda e: e.memset(onesf, 1.0 / 1024), writes=['onesf'])
        def diag_build(cc):
            par = cc % 2
            dg = diag[par]
            t.op('dve', lambda e: e.tensor_tensor(
                out=dg, in0=ident.unsqueeze(1).to_broadcast([128, 31, 128]),
                in1=wdw[:, cc, :].unsqueeze(2).to_broadcast([128, 31, 128]), op=ALU.mult),
                reads=['cb', 'cf'], writes=[('diag', par)])

        def conv_mm(th, cc):
            t0 = 512 * th
            par = cc % 2
            dg = diag[par]
            nxt = cc + 1 if cc < 7 else (0 if th == 0 else None)
            if nxt is not None:
                pass
            for j in range(CONVW):
                t.op('pe', lambda e, j=j: e.matmul(
                    bank(par), lhsT=dg[:, j, :], rhs=uT[:, cc, t0 + 2 + j:t0 + 2 + j + 512], start=(j == 0), stop=(j == CONVW - 1)),
                    reads=[('diag', par)] + [('uT', cc, g) for g in range(3)], writes=[('ps', par)], sig=(j == CONVW - 1))

        def evac_act(th, cc):
            par = cc % 2
            t.op('act', lambda e: e.activation(out=yT[:, cc, :], in_=bank(par), func=AF.Identity, bias=bdw[:, cc:cc + 1]),
                 reads=[('ps', par), 'cf'], writes=[('yT', cc)] + ([('sgs', 0), ('sgs', 1)] if cc < 2 else []))
            t.op('act', lambda e: e.activation(out=ysq[par], in_=yT[:, cc, :], func=AF.Square),
                 reads=[('yT', cc)], writes=[('ysq', par)])

        def stats_mm(th, cc):
            par = cc % 2
            t.op('pe', lambda e: e.matmul(bank(2), lhsT=onesf, rhs=yT[:, cc, :], start=(cc == 0), stop=(cc == 7)),
                 reads=['onesf', ('yT', cc)], writes=[('ps', 2)], sig=True)
            t.op('pe', lambda e: e.matmul(bank(3), lhsT=onesf, rhs=ysq[par], start=(cc == 0), stop=(cc == 7)),
                 reads=['onesf', ('ysq', par)], writes=[('ps', 3)], sig=True)

        def stats_finish():
            t.op('act', lambda e: e.activation(out=mean_b, in_=bank(2), func=AF.Copy), reads=[('ps', 2)], writes=['mean_b'])
            t.op('dve', lambda e: e.tensor_tensor(out=var_b, in0=mean_b, in1=mean_b, op=ALU.mult), reads=['mean_b'], writes=['var_b'])
            t.op('dve', lambda e: e.tensor_tensor(out=var_b, in0=bank(3), in1=var_b, op=ALU.subtract), reads=[('ps', 3), 'var_b'], writes=['var_b'])
            t.op('act', lambda e: e.activation(out=var_b, in_=var_b, func=AF.Sqrt, bias=eps5), reads=['var_b', 'eps5'], writes=['var_b'])
            t.op('dve', lambda e: e.reciprocal(out=var_b, in_=var_b), reads=['var_b'], writes=['var_b'])

        def epi_a(th, cc):
            t.op('dve', lambda e: e.tensor_tensor(out=yT[:, cc, :], in0=yT[:, cc, :], in1=mean_b, op=ALU.subtract),
                 reads=[('yT', cc), 'mean_b'], writes=[('yT', cc)])
            t.op('dve', lambda e: e.tensor_tensor(out=yT[:, cc, :], in0=yT[:, cc, :], in1=var_b, op=ALU.mult),
                 reads=[('yT', cc), 'var_b'], writes=[('yT', cc)])
            t.op('act', lambda e: e.activation(out=yT[:, cc, :], in_=yT[:, cc, :], func=AF.Silu,
                                               scale=lng[:, cc:cc + 1], bias=lnb[:, cc:cc + 1]),
                 reads=[('yT', cc), 'cf'], writes=[('yT', cc)])

        def epi_b(th, cc):
            t0 = 512 * th
            t.op('dve', lambda e: e.tensor_tensor(out=u2T[:, cc, t0:t0 + 512], in0=yT[:, cc, :],
                                                  in1=zcT[:, cc, t0:t0 + 512], op=ALU.mult),
                 reads=[('yT', cc), ('zcT', cc, th)], writes=[('u2T', cc, th)])

        diag_build(0)
        for cc in range(8):
            diag_build((cc + 1) % 8)
            conv_mm(0, cc)
            if cc > 0:
                stats_mm(0, cc - 1)
            evac_act(0, cc)
        stats_mm(0, 7)
        stats_finish()
        for cc in range(8):
            if cc < 7:
                diag_build(cc + 1)
            conv_mm(1, cc)
            if cc > 0:
                stats_mm(1, cc - 1)
            epi_a(0, cc)
            epi_b(0, cc)
            evac_act(1, cc)
        stats_mm(1, 7)
        stats_finish()
        epi_queue = []
        for cc in range(9):
            if cc < 8:
                epi_queue.append(lambda cc=cc: epi_a(1, cc))
            if cc > 0:
                epi_queue.append(lambda cc=cc: epi_b(1, cc - 1))
        if debug:
            while epi_queue:
                epi_queue.pop(0)()
        if debug:
            t.dma('sp', dbg["d_hT"], hT_t[:, :], reads=hT_all, key=('dbg', 1))
            t.dma('sp', dbg["d_uT"], uT.rearrange("p a n -> p (a n)"), reads=uT_all, key=('dbg', 2))
            t.dma('sp', dbg["d_zcT"], zcT.rearrange("p a n -> p (a n)"), reads=[('zcT', cc, tg) for cc in range(8) for tg in range(2)], key=('dbg', 3))
            t.dma('sp', dbg["d_u2T"], u2T_t[:, :], reads=[('u2T', cc, th) for cc in range(8) for th in range(2)], key=('dbg', 4))

        if stop_after == 'E3':
            return finish()
        def bcd_load(hg, which):
            col = {'k': C_K, 'v': C_V, 'q': C_Q, 'z': C_ZA}[which] + hg * 512
            u = {'k': 0, 'v': 2, 'q': 0, 'z': 2}[which]
            r = load_w(u * 4096, 8192, w_in_v[:, :, col:col + 512], ('Wk', u))
            return u, r

        fkc = [0]
        for hg in range(2):
            pk = bcd_load(hg, 'k')
            pv = bcd_load(hg, 'v')
            ozT4 = ozT_t[:, :].rearrange("p (i j n) -> p i j n", i=8, j=4)
            zcol = C_ZA + hg * 512
            za_res = [('oz', i_, h_) for i_ in range(8) for h_ in range(4, 8)]
            for i2 in range(8):
                tokz = t.dma('pool', ozT4[:, i2, 2:4, :], w_in_v[:, 2 * i2:2 * i2 + 2, zcol:zcol + 256],
                             writes=[('oz', i2, h_) for h_ in range(4, 8)], key=('za', hg))
            for r_ in za_res:
                t.lastw[r_] = tokz
            if hg == 0:
                zb_res = [('oz', i_, h_) for i_ in range(8) for h_ in range(0, 4)]
                for i2 in range(8):
                    tokz = t.dma('pool', ozT4[:, i2, 0:2, :], w_in_v[:, 2 * i2:2 * i2 + 2, zcol + 256:zcol + 512],
                                 writes=[('oz', i2, h_) for h_ in range(0, 4)], key=('zb', hg))
                for r_ in zb_res:
                    t.lastw[r_] = tokz
            if debug:
                t.wait_dbg()
            A.reset()
            KT = A.bf(4 * 2048).rearrange("p (a n) -> p a n", a=4)
            QT = A.bf(4 * 1024).rearrange("p (a n) -> p a n", a=4)
            vaug = A.bf(16 * 4 * 130).rearrange("p (m h e) -> p m h e", m=16, h=4)
            zsb = A.bf(8 * 512).rearrange("p (a n) -> p a n", a=8)
            ET = [A.bf(1024).rearrange("p (c n) -> p c n", c=2) for _ in range(3)]
            t0s = [A.f32(128) for _ in range(4)]
            osb = [A.f32(128) for _ in range(4)]
            ztm = [A.f32(512), A.f32(512)]
            ktm = [A.bf(512), A.bf(512)]
            rt = [A.f32(64).rearrange("p (g d) -> p g d", g=8) for _ in range(4)]
            assert A.off - 1536 >= 28928 + 512, A.off

            def bcd_setup():
                t.op('dve', lambda e: e.tensor_copy(out=vaug[:, :, :, 128:129], in_=vones.unsqueeze(2).unsqueeze(3).to_broadcast([128, 16, 4, 1])),
                     reads=['cb'], writes=['vones'])
                t.op('dve', lambda e: e.memset(vaug[:, :, :, 129:130], 0.0), writes=['vpad'])
            if hg == 1:
                bcd_setup()

            def proj(tb, wpanel, rw, b):
                for dc in range(16):
                    t.op('pe', lambda e, dc=dc, tb=tb, wpanel=wpanel, b=b: e.matmul(
                        bank(b), lhsT=hT[:, dc, tb * 128:(tb + 1) * 128], rhs=wpanel[:, dc, :], start=(dc == 0), stop=(dc == 15)),
                        reads=rw + [('hT', tb, 0), ('hT', tb, 1)], writes=[('ps', b)], sig=(dc == 15))

            def rotary(b, tb, kt, kres):
                src = bank(b).rearrange("p (g d) -> p g d", g=8)
                dst = kt.rearrange("p (g d) -> p g d", g=8)
                cb_ = cosv[:, tb, :].unsqueeze(1).to_broadcast([128, 8, 8])
                sb_ = sinv[:, tb, :].unsqueeze(1).to_broadcast([128, 8, 8])
                t.op('act', lambda e: e.activation(out=kt, in_=bank(b), func=AF.Copy), reads=[('ps', b)], writes=[kres])
                x1 = src[:, :, 0:8]
                x2 = src[:, :, 8:16]
                t.op('dve', lambda e: e.tensor_tensor(out=rt[0], in0=x1, in1=cb_, op=ALU.mult), reads=[('ps', b), 'cf', kres], writes=['rt0'])
                t.op('dve', lambda e: e.tensor_tensor(out=rt[1], in0=x2, in1=sb_, op=ALU.mult), reads=[('ps', b), 'cf', kres], writes=['rt1'])
                t.op('dve', lambda e: e.tensor_tensor(out=rt[2], in0=x2, in1=cb_, op=ALU.mult), reads=[('ps', b), 'cf', kres], writes=['rt2'])
                t.op('dve', lambda e: e.tensor_tensor(out=rt[3], in0=x1, in1=sb_, op=ALU.mult), reads=[('ps', b), 'cf', kres], writes=['rt3'])
                t.op('dve', lambda e: e.tensor_tensor(out=dst[:, :, 0:8], in0=rt[0], in1=rt[1], op=ALU.subtract),
                     reads=['rt0', 'rt1'], writes=[kres])
                t.op('dve', lambda e: e.tensor_tensor(out=dst[:, :, 8:16], in0=rt[2], in1=rt[3], op=ALU.add),
                     reads=['rt2', 'rt3'], writes=[kres])

            u, rw = pk
            wk = wring[:, u * 4096:(u + 2) * 4096].rearrange("p (a n) -> p a n", a=16)

            def k_stage1(tb):
                par = tb % 2
                proj(tb, wk, rw, par)
                rotary(par, tb, ktm[par], ('ktm', par))

            def k_stage2(tb):
                par = tb % 2
                tbk = 2 + par
                for hl in range(4):
                    o = bank_bf(tbk)[:, hl * 128:(hl + 1) * 128]
                    t.op('pe', lambda e, o=o, par=par, hl=hl: e.transpose(out=o, in_=ktm[par][:, hl * 128:(hl + 1) * 128], identity=ident),
                         reads=[('ktm', par), 'cb'], writes=[('ps', tbk)], sig=(hl == 3))
                srcT = bank_bf(tbk)[:, 0:512].rearrange("p (a n) -> p a n", a=4)
                t.op('dve' if par else 'act',
                     (lambda e, srcT=srcT, tb=tb: e.tensor_copy(out=KT[:, :, tb * 128:(tb + 1) * 128], in_=srcT)) if par else
                     (lambda e, srcT=srcT, tb=tb: e.activation(out=KT[:, :, tb * 128:(tb + 1) * 128], in_=srcT, func=AF.Copy)),
                     reads=[('ps', tbk)], writes=[('KT', tb)])
            for tb in range(17):
                for _ in range(3):
                    if epi_queue:
                        epi_queue.pop(0)()
                if tb < 16:
                    k_stage1(tb)
                if tb > 0:
                    k_stage2(tb - 1)
            assert not epi_queue
            if hg == 0:
                t.barrier()
                bcd_setup()
            if stop_after == 'B1':
                return finish()
            pq = bcd_load(hg, 'q')
            u, rw = pv
            wvp = wring[:, u * 4096:(u + 2) * 4096].rearrange("p (a n) -> p a n", a=16)
            for tb in range(16):
                par = tb % 2
                proj(tb, wvp, rw, par)
                srcV = bank(par).rearrange("p (h e) -> p h e", h=4)
                t.op('dve' if par else 'act',
                     (lambda e, srcV=srcV, tb=tb: e.tensor_copy(out=vaug[:, tb, :, 0:128], in_=srcV)) if par else
                     (lambda e, srcV=srcV, tb=tb: e.activation(out=vaug[:, tb, :, 0:128], in_=srcV, func=AF.Copy)),
                     reads=[('ps', par)], writes=[('vaug', tb)])
            if stop_after == 'B2':
                return finish()
            if hg == 1:
                zb_res = load_w(2 * 4096, 4096, w_in_v[:, :, zcol + 256:zcol + 512], ('Wk', 2))
            if hg == 1:
                woutA = hT[:, :, 0:1024]
                prev_res = [('hT', tb_, k_) for tb_ in range(8) for k_ in range(2)]
                for q4 in range(4):
                    tokA = t.dma('pool', woutA[:, 4 * q4:4 * q4 + 4, :], w_out_v[:, 4 * q4:4 * q4 + 4, 0:1024],
                                 writes=[('woutA', q4)] + prev_res, key='woutA')
                for q4 in range(4):
                    t.lastw[('woutA', q4)] = tokA
            u, rw = pq
            wq = wring[:, u * 4096:(u + 2) * 4096].rearrange("p (a n) -> p a n", a=16)

            def q_stage1(i):
                par = i % 2
                proj(8 + i, wq, rw, par)
                rotary(par, 8 + i, ktm[par], ('ktm', par))

            def q_stage2(i):
                par = i % 2
                tbk = 2 + par
                for hl in range(4):
                    o = bank_bf(tbk)[:, hl * 128:(hl + 1) * 128]
                    t.op('pe', lambda e, o=o, par=par, hl=hl: e.transpose(out=o, in_=ktm[par][:, hl * 128:(hl + 1) * 128], identity=ident),
                         reads=[('ktm', par), 'cb'], writes=[('ps', tbk)], sig=(hl == 3))
                srcT = bank_bf(tbk)[:, 0:512].rearrange("p (a n) -> p a n", a=4)
                t.op('dve' if par else 'act',
                     (lambda e, srcT=srcT, i=i: e.tensor_copy(out=QT[:, :, i * 128:(i + 1) * 128], in_=srcT)) if par else
                     (lambda e, srcT=srcT, i=i: e.activation(out=QT[:, :, i * 128:(i + 1) * 128], in_=srcT, func=AF.Copy)),
                     reads=[('ps', tbk)], writes=[('QT', i)])
            for i in range(9):
                if i < 8:
                    q_stage1(i)
                if i > 0:
                    q_stage2(i - 1)
            if stop_after == 'C1':
                return finish()
            zb_ring = wring[:, 2 * 4096:3 * 4096].rearrange("p (a n) -> p a n", a=16)

            def za_ap(dc):
                return ozT4[:, dc // 2, 2 + dc % 2, :]

            def zb_ap(dc):
                return ozT4[:, dc // 2, dc % 2, :] if hg == 0 else zb_ring[:, dc, :]
            for i in range(8):
                tb = 8 + i
                par = i % 2
                for half, getw, wres in ((0, za_ap, za_res), (1, zb_ap, zb_res)):
                    for dc in range(16):
                        wap = getw(dc)
                        t.op('pe', lambda e, dc=dc, tb=tb, par=par, half=half, wap=wap: e.matmul(
                            bank(par, 256, 256 * half), lhsT=hT[:, dc, tb * 128:(tb + 1) * 128], rhs=wap, start=(dc == 0), stop=(dc == 15)),
                            reads=wres + [('hT', tb, 0), ('hT', tb, 1)], writes=[('ps', par)], sig=(dc == 15))
                t.op('act', lambda e, par=par: e.activation(out=ztm[par], in_=bank(par), func=AF.Silu), reads=[('ps', par)], writes=[('ztm', par)])
                t.op('dve', lambda e, par=par, i=i: e.tensor_tensor(
                    out=zsb[:, i, :].rearrange("p (h e) -> p h e", h=4), in0=ztm[par].rearrange("p (h e) -> p h e", h=4),
                    in1=sublnw.unsqueeze(1).to_broadcast([128, 4, 128]), op=ALU.mult),
                    reads=[('ztm', par), 'sublnw'], writes=[('zsb', i)])
            if debug and hg == 0:
                t.dma('sp', dbg["d_KT"], KT.rearrange("p a n -> p (a n)"), reads=[('KT', tb) for tb in range(16)], key=('dbg', 5))
                t.dma('sp', dbg["d_Q0"], QT.rearrange("p a n -> p (a n)"), reads=[('QT', i) for i in range(8)], key=('dbg', 6))
                t.dma('sp', dbg["d_vaug"], vaug.rearrange("p m h e -> p (m h e)"), reads=[('vaug', tb) for tb in range(16)] + ['vones', 'vpad'], key=('dbg', 7))
                t.dma('sp', dbg["d_zsb"], zsb.rearrange("p a n -> p (a n)"), reads=[('zsb', i) for i in range(8)], key=('dbg', 8))

            if stop_after == 'C2':
                return finish()
            def acc_ap(ii, c, n=129, off=0):
                return bank(4 + ii, n, c * 129 + off), ('ps', 4 + ii)

            steps = []
            for hl in range(4):
                for g in range(2):
                    for m in range(8 + 4 * g + 4):
                        steps.append((hl, g, m))

            def emit_qk_exp(si):
                hl, g, m = steps[si]
                s0 = 0 if m < 8 + 4 * g else (m - 8 - 4 * g) * 128
                n = 512 - s0
                et = ET[si % 3]
                eres = ('ET', si % 3)
                pb = 2 * (si % 2)
                qres = [('QT', i) for i in range(4 * g, 4 * g + 4)]
                qres1 = qres
                diag_step = (m >= 8 + 4 * g)
                if WARM_JUNK:
                    for jb in range(WARM_JUNK):
                        t.op('pe', lambda e, jb=jb: e.matmul(bank(pb + jb % 2), lhsT=KT[:, hl, m * 128:(m + 1) * 128],
                                                            rhs=QT[:, hl, g * 512:g * 512 + 512], start=True, stop=True),
                             reads=[('KT', m)] + qres, writes=[('ps', pb + jb % 2)], sig=False)
                t.op('pe', lambda e: e.matmul(bank(pb, n, s0), lhsT=KT[0:64, hl, m * 128:(m + 1) * 128],
                                              rhs=QT[0:64, hl, g * 512 + s0:g * 512 + 512], start=True, stop=not diag_step),
                     reads=[('KT', m)] + qres, writes=[('ps', pb)], sig=False)
                t.op('pe', lambda e: e.matmul(bank(pb + 1, n, s0), lhsT=KT[64:128, hl, m * 128:(m + 1) * 128],
                                              rhs=QT[64:128, hl, g * 512 + s0:g * 512 + 512], start=True, stop=not diag_step),
                     reads=[('KT', m)] + qres1, writes=[('ps', pb + 1)], sig=not diag_step)
                if diag_step:
                    t.op('pe', lambda e: e.matmul(bank(pb, 128, s0), lhsT=ident, rhs=tri, start=False, stop=True),
                         reads=['cb'], writes=[('ps', pb)], sig=False)
                    t.op('pe', lambda e: e.matmul(bank(pb + 1, 128, s0), lhsT=ident, rhs=tri, start=False, stop=True),
                         reads=['cb'], writes=[('ps', pb + 1)], sig=True)
                src = ps[:, 512 * pb:512 * pb + 1024].rearrange("p (c n) -> p c n", c=2)[:, :, s0:512]
                t.op('act', lambda e: e.activation(out=et[:, :, s0:512], in_=src, func=AF.Exp, scale=0.125),
                     reads=[('ps', pb), ('ps', pb + 1)], writes=[eres])

            def emit_pv(si):
                global_fk = fkc[0]
                hl, g, m = steps[si]
                h = 4 * hg + hl
                s0 = 0 if m < 8 + 4 * g else (m - 8 - 4 * g) * 128
                et = ET[si % 3]
                eres = ('ET', si % 3)
                for ii in range(s0 // 128, 4):
                    last = (m == 8 + 4 * g + ii)
                    for c in range(2):
                        oap, ores = acc_ap(ii, c)
                        t.op('pe', lambda e, oap=oap, c=c, ii=ii, last=last: e.matmul(
                            oap, lhsT=et[:, c, ii * 128:(ii + 1) * 128], rhs=vaug[:, m, hl, 0:129],
                            start=(m == 0 and c == 0), stop=(last and c == 1)),
                            reads=[eres, ('vaug', m), 'vones'], writes=[ores], sig=(last and c == 1))
                    if last:
                        finalize(h, hl, 4 * g + ii, ii)

            def finalize(h, hl, i, ii):
                fk = fkc[0]
                fkc[0] += 1
                st = fin[:, fk, :]
                fres = ('fin', fk)
                fp = fk % 4
                o0, r0 = acc_ap(ii, 0, 128)
                o1, r1 = acc_ap(ii, 1, 128)
                s0a, _ = acc_ap(ii, 0, 1, 128)
                s1a, _ = acc_ap(ii, 1, 1, 128)
                t.op('dve', lambda e: e.reciprocal(out=st[:, 0:1], in_=s0a), reads=[r0, 'small'], writes=[fres + (0,)])
                t.op('dve', lambda e: e.reciprocal(out=st[:, 1:2], in_=s1a), reads=[r1, 'small'], writes=[fres + (1,)])
                t.op('dve', lambda e: e.tensor_tensor(out=st[:, 1:2], in0=st[:, 1:2], in1=neglam, op=ALU.mult),
                     reads=[fres + (1,), 'neglam'], writes=[fres + (1,)])
                t.op('dve', lambda e: e.tensor_scalar(out=t0s[fp], in0=o0, scalar1=st[:, 0:1], scalar2=None, op0=ALU.mult),
                     reads=[r0, fres + (0,)], writes=[('t0s', fp)])
                t.op('dve', lambda e: e.scalar_tensor_tensor(out=osb[fp], in0=o1, scalar=st[:, 1:2], in1=t0s[fp], op0=ALU.mult, op1=ALU.add),
                     reads=[r1, fres + (1,), ('t0s', fp)], writes=[('osb', fp)])
                def stage1b():
                    t.op('dve', lambda e: e.scalar_tensor_tensor(out=t0s[fp], in0=osb[fp], scalar=1.0, in1=osb[fp], op0=ALU.mult, op1=ALU.mult,
                                                                 accum_out=st[:, 2:3]),
                         reads=[('osb', fp), 'small'], writes=[('t0s', fp), fres + (2,)])
                pending_b.append((cur[0], stage1b))

                def stage2():
                    t.op('act', lambda e: e.activation(out=st[:, 3:4], in_=st[:, 2:3], func=AF.Ln, scale=1.0 / 128, bias=eps5),
                         reads=[fres + (2,), 'eps5'], writes=[fres + (3,)])
                    t.op('act', lambda e: e.activation(out=st[:, 4:5], in_=st[:, 3:4], func=AF.Exp, scale=-0.5),
                         reads=[fres + (3,)], writes=[fres + (4,)])
                    t.op('dve', lambda e: e.scalar_tensor_tensor(
                        out=ozT[:, i, h * 128:(h + 1) * 128], in0=osb[fp], scalar=st[:, 4:5], in1=zsb[:, i, hl * 128:(hl + 1) * 128],
                        op0=ALU.mult, op1=ALU.mult),
                        reads=[('osb', fp), fres + (4,), ('zsb', i)], writes=[('oz', i, h)])
                pending.append((cur[0], stage2))

            pending = []
            pending_b = []
            cur = [0]

            def flush(upto):
                while pending_b and pending_b[0][0] <= upto + FIN_DELAY - 1:
                    pending_b.pop(0)[1]()
                while pending and pending[0][0] <= upto - 1:
                    pending.pop(0)[1]()

            emit_qk_exp(0)
            emit_qk_exp(1)
            for si in range(len(steps)):
                cur[0] = si
                if si + 2 < len(steps):
                    emit_qk_exp(si + 2)
                flush(si - FIN_DELAY)
                emit_pv(si)
            flush(len(steps))
        ozT_all = [('oz', a_, b_) for a_ in range(8) for b_ in range(8)]

        if stop_after == 'D':
            return finish()
        def f_load(pi):
            ua = (pi % 2) * 2
            rga = load_w(ua * 4096, 4096, w_in_v[:, :, C_GA + pi * 256:C_GA + pi * 256 + 256], ('Wk', ua))
            return ua, rga

        def f_load2(pi):
            ua = (pi % 2) * 2
            rgc = load_w((ua + 1) * 4096, 4096, w_in_v[:, :, C_GC + pi * 256:C_GC + pi * 256 + 256], ('Wk', ua + 1))
            return rgc
        pend = f_load(0)
        pend2 = f_load2(0)
        t.barrier()
        A.reset()
        mT = A.bf(16 * 1024).rearrange("p (a n) -> p a n", a=16)
        slots = [(a_, a_) for a_ in range(8)]
        for a_ in range(8):
            for b_ in range(a_ + 1, 8):
                slots += [(a_, b_), (b_, a_)]
        for r0 in range(0, 64, 8):
            bk = (r0 // 8) % 4
            grp = slots[r0:r0 + 8]
            for k, (a_, b_) in enumerate(grp):
                o = bank_bf(bk)[:, k * 128:(k + 1) * 128]
                t.op('pe', lambda e, o=o, a_=a_, b_=b_: e.transpose(out=o, in_=ozT[:, a_, b_ * 128:(b_ + 1) * 128], identity=ident),
                     reads=[('oz', a_, b_), 'cb'], writes=[('ps', bk)], sig=(k == 7))
            for k, (a_, b_) in enumerate(grp):
                o = bank_bf(bk)[:, k * 128:(k + 1) * 128]
                if bk % 2 == 0:
                    t.op('act', lambda e, o=o, a_=a_, b_=b_: e.activation(out=ozT[:, b_, a_ * 128:(a_ + 1) * 128], in_=o, func=AF.Copy),
                         reads=[('ps', bk)], writes=[('oz', b_, a_)])
                else:
                    t.op('dve', lambda e, o=o, a_=a_, b_=b_: e.tensor_copy(out=ozT[:, b_, a_ * 128:(a_ + 1) * 128], in_=o),
                         reads=[('ps', bk)], writes=[('oz', b_, a_)])
        if debug:
            t.dma('sp', dbg["d_ozT"], ozT_t[:, :], reads=ozT_all, key=('dbg', 9))
        wop = [A.bf(4096), A.bf(4096)]
        sga = A.f32(512)
        sgc = A.f32(512)
        f1 = A.f32(512)
        f2 = A.f32(512)
        wvs = A.f32(D)
        t.dma('sp', wvs, wvec_d[:, D:2 * D], writes=['wv2'], key='wv2')

        def wop_load(pi):
            sl = wop[pi % 2]
            dst = sl.rearrange("p (w a n) -> p w a n", w=2, a=8)
            k = ('wop', pi % 2)
            t.dma('pool', dst[:, 0, :, :], w_o_v[:, :, pi * 256:(pi + 1) * 256], writes=[('wop', pi % 2, 0)], key=k)
            tok = t.dma('pool', dst[:, 1, :, :], w_pw2_v[:, :, pi * 256:(pi + 1) * 256], writes=[('wop', pi % 2, 1)], key=k)
            t.lastw[('wop', pi % 2, 0)] = tok
        wop_load(0)
        def woutB_load(q4):
            dst = wring[:, 4096 * q4:4096 * (q4 + 1)].rearrange("p (a n) -> p a n", a=4)
            t.dma('pool', dst, w_out_v[:, 4 * q4:4 * q4 + 4, 1024:2048],
                  writes=[('W', q) for q in range(4 * q4, 4 * q4 + 4)], key=('woutB', q4))

        for pi in range(8):
            ua, rga = pend
            rgc = pend2
            if pi == 7:
                woutB_load(0)
                woutB_load(1)
            wga = wring[:, ua * 4096:(ua + 1) * 4096].rearrange("p (a n) -> p a n", a=16)
            wgc = wring[:, (ua + 1) * 4096:(ua + 2) * 4096].rearrange("p (a n) -> p a n", a=16)
            wsl = wop[pi % 2].rearrange("p (w a n) -> p w a n", w=2, a=8)
            if pi + 1 < 8:
                pend = f_load(pi + 1)
                pend2 = f_load2(pi + 1)
                wop_load(pi + 1)
            for ci in range(2):
                nch = pi * 2 + ci
                for tg in range(2):
                    par = it % 2
                    it += 1
                    b0 = 4 * par
                    tc0 = 1024 + tg * 512
                    for dc in range(16):
                        t.op('pe', lambda e, dc=dc, b0=b0, wga=wga, ci=ci, tc0=tc0: e.matmul(
                            bank(b0), lhsT=wga[:, dc, ci * 128:(ci + 1) * 128], rhs=hT[:, dc, tc0:tc0 + 512], start=(dc == 0), stop=(dc == 15)),
                            reads=rga + hT_own, writes=[('ps', b0)], sig=(dc == 15))
                    for dc in range(16):
                        t.op('pe', lambda e, dc=dc, b0=b0, wgc=wgc, ci=ci, tc0=tc0: e.matmul(
                            bank(b0 + 1), lhsT=wgc[:, dc, ci * 128:(ci + 1) * 128], rhs=hT[:, dc, tc0:tc0 + 512], start=(dc == 0), stop=(dc == 15)),
                            reads=rgc + hT_own, writes=[('ps', b0 + 1)], sig=(dc == 15))
                    for fc in range(8):
                        t.op('pe', lambda e, fc=fc, b0=b0, wsl=wsl, ci=ci, tg=tg: e.matmul(
                            bank(b0 + 2), lhsT=wsl[:, 0, fc, ci * 128:(ci + 1) * 128], rhs=ozT[:, fc, tg * 512:(tg + 1) * 512], start=(fc == 0), stop=(fc == 7)),
                            reads=[('wop', pi % 2, 0)] + ozT_all, writes=[('ps', b0 + 2)], sig=(fc == 7))
                    for fc in range(8):
                        t.op('pe', lambda e, fc=fc, b0=b0, wsl=wsl, ci=ci, tg=tg: e.matmul(
                            bank(b0 + 3), lhsT=wsl[:, 1, fc, ci * 128:(ci + 1) * 128], rhs=u2T[:, fc, tg * 512:(tg + 1) * 512], start=(fc == 0), stop=(fc == 7)),
                            reads=[('wop', pi % 2, 1)] + [('u2T', fc, tg) for fc in range(8)], writes=[('ps', b0 + 3)], sig=(fc == 7))
                    t.op('act', lambda e, b0=b0: e.activation(out=sga, in_=bank(b0), func=AF.Sigmoid), reads=[('ps', b0)], writes=['sga'])
                    t.op('act', lambda e, b0=b0: e.activation(out=sgc, in_=bank(b0 + 1), func=AF.Sigmoid), reads=[('ps', b0 + 1)], writes=['sgc'])
                    t.op('dve', lambda e, b0=b0: e.tensor_tensor(out=f1, in0=bank(b0 + 2), in1=sga, op=ALU.mult), reads=[('ps', b0 + 2), 'sga'], writes=['f1'])
                    t.op('dve', lambda e, b0=b0, nch=nch: e.scalar_tensor_tensor(out=f2, in0=bank(b0 + 3), scalar=bpw2[:, nch:nch + 1], in1=sgc,
                                                                               op0=ALU.add, op1=ALU.mult),
                         reads=[('ps', b0 + 3), 'sgc', 'cf'], writes=['f2'])
                    t.op('dve', lambda e, nch=nch, tg=tg: e.tensor_tensor(out=mT[:, nch, tg * 512:(tg + 1) * 512], in0=f1, in1=f2, op=ALU.add),
                         reads=['f1', 'f2'], writes=[('mT', nch, tg)])
        if debug:
            t.dma('sp', dbg["d_mT"], mT.rearrange("p a n -> p (a n)"), reads=[('mT', n_, g_) for n_ in range(16) for g_ in range(2)], key=('dbg', 10))

        if stop_after == 'F':
            return finish()
        u2T_all = [('u2T', cc, th) for cc in range(8) for th in range(2)]
        woutB = wring.rearrange("p (a n) -> p a n", a=16)
        woutB_load(2)
        woutB_load(3)
        xr = [ozT_t[:, 0:4096].bitcast(F32), ozT_t[:, 4096:8192].bitcast(F32)]
        ost = [u2T_t[:, 0:4096].bitcast(F32), u2T_t[:, 4096:8192].bitcast(F32)]
        for i in range(8):
            par = i % 2
            b0 = 4 * par
            t.dma('sp', xr[par], x_all[1024 + i * 128:1024 + (i + 1) * 128, :],
                  writes=[('xr', par)] + (ozT_all if i < 2 else []), key=('xr', par))
            for npn in range(4):
                c0 = (npn % 2) * 512
                for dc in range(16):
                    wsrc = woutA if npn < 2 else woutB
                    wres = [('woutA', dc // 4)] if npn < 2 else [('W', q) for q in range(dc, dc + 1)]
                    t.op('pe', lambda e, dc=dc, b0=b0, npn=npn, wsrc=wsrc, c0=c0, i=i: e.matmul(
                        bank(b0 + npn), lhsT=mT[:, dc, i * 128:(i + 1) * 128], rhs=wsrc[:, dc, c0:c0 + 512], start=(dc == 0), stop=(dc == 15)),
                        reads=wres + [('mT', dc, i // 4)], writes=[('ps', b0 + npn)], sig=(dc == 15))
            pall = ps[:, 512 * b0:512 * b0 + 2048]
            pres = [('ps', b0 + k) for k in range(4)]
            ores = [('ostA', par), ('ostB', par)]
            t.op('act', lambda e, pall=pall, i=i, par=par: e.activation(out=ost[par], in_=pall, func=AF.Square, accum_out=ssq2[:, i:i + 1]),
                 reads=pres + ['small'], writes=ores + [('ssq2', i)] + (u2T_all if i < 2 else []))
            t.op('act', lambda e, i=i: e.activation(out=sq2[:, i:i + 1], in_=ssq2[:, i:i + 1], func=AF.Sqrt, scale=1.0 / D, bias=eps6),
                 reads=[('ssq2', i), 'eps6'], writes=[('sq2', i)])
            t.op('dve', lambda e, i=i: e.reciprocal(out=rstd2[:, i:i + 1], in_=sq2[:, i:i + 1]), reads=[('sq2', i)], writes=[('rstd2', i)])
            for hh in range(2):
                c0 = 1024 * hh
                ph = [('ps', b0 + 2 * hh), ('ps', b0 + 2 * hh + 1)]
                t.op('dve', lambda e, pall=pall, i=i, par=par, c0=c0: e.scalar_tensor_tensor(
                    out=ost[par][:, c0:c0 + 1024], in0=pall[:, c0:c0 + 1024], scalar=rstd2[:, i:i + 1], in1=wvs[:, c0:c0 + 1024],
                    op0=ALU.mult, op1=ALU.mult), reads=ph + [('rstd2', i), 'wv2'], writes=[ores[hh]])
                t.op('dve', lambda e, par=par, c0=c0: e.tensor_tensor(out=ost[par][:, c0:c0 + 1024], in0=ost[par][:, c0:c0 + 1024],
                                                                     in1=xr[par][:, c0:c0 + 1024], op=ALU.add),
                     reads=[('xr', par), ores[hh]], writes=[ores[hh]])
                t.dma('sp', out_d[i * 128:(i + 1) * 128, c0:c0 + 1024], ost[par][:, c0:c0 + 1024], reads=[ores[hh]], key=('out', par, hh))
        keys = [('out', p_, h_) for p_ in range(2) for h_ in range(2)] + [k for k in t.dcnt if isinstance(k, tuple) and k[0] == 'dbg']
        t.wait_all_dma('sp', keys)
        print("instr counts:", {e: len(t.prog[e]) for e in t.ENG}, "sems:", len(t.dsem) + 5)
        t.emit()
    return nc


def host_consts(inputs, half):
    cf = np.zeros((128, CF_N), np.float32)
    own0 = half * T
    pos = (own0 - T) + np.arange(2048, dtype=np.float64)
    inv = ROPE_THETA ** (-np.arange(0, 16, 2, dtype=np.float64) / 16)
    ang = (pos.astype(np.float32)[:, None] * inv.astype(np.float32)[None, :]).astype(np.float32).astype(np.float64)
    cos = np.cos(ang).astype(np.float32).reshape(16, 128, 8).transpose(1, 0, 2).reshape(128, 128)
    sin = np.sin(ang).astype(np.float32).reshape(16, 128, 8).transpose(1, 0, 2).reshape(128, 128)
    cf[:, CF_COS:CF_COS + 128] = cos
    cf[:, CF_SIN:CF_SIN + 128] = sin
    cf[:, CF_BGLU:CF_BGLU + 16] = inputs['b_glu'][0].reshape(16, 128).T
    cf[:, CF_BDW:CF_BDW + 8] = inputs['b_dw'][0].reshape(8, 128).T
    cf[:, CF_LNG:CF_LNG + 8] = inputs['ln_g'][0].reshape(8, 128).T
    cf[:, CF_LNB:CF_LNB + 8] = inputs['ln_b'][0].reshape(8, 128).T
    cf[:, CF_BPW2:CF_BPW2 + 16] = inputs['b_pw2'][0].reshape(16, 128).T
    cf[:, CF_WDW:CF_WDW + 248] = inputs['w_dw'][0].reshape(31, 8, 128).transpose(2, 1, 0).reshape(128, 248)
    lam = np.concatenate([inputs['lambda_q1'][0], inputs['lambda_k1'][0], inputs['lambda_q2'][0], inputs['lambda_k2'][0]])
    cf[:, CF_LAM:CF_LAM + 256] = lam[None, :]
    cf[:, CF_SUBLN:CF_SUBLN + 128] = inputs['subln_w'][0][None, :]
    cf[:, CF_HALO] = float(half)
    cb = np.zeros((128, CB_N), np.float32)
    cb[:, CB_ID:CB_ID + 128] = np.eye(128)
    kk = np.arange(128)[:, None]
    qq = np.arange(128)[None, :]
    cb[:, CB_TRI:CB_TRI + 128] = np.where(kk <= qq, 0.0, -30000.0)
    cb[:, CB_VONES:CB_VONES + 8] = float(half)
    cb[:, CB_VONES + 8:CB_VONES + 16] = 1.0
    return cf, cb.astype(ml_dtypes.bfloat16)


def make_in_maps(inputs):
    x = np.asarray(inputs['x'], np.float32)
    w_in = np.ascontiguousarray(np.asarray(inputs['w_in'], np.float32)[0])
    w_o = np.ascontiguousarray(np.asarray(inputs['w_o_attn'], np.float32)[0])
    w_pw2 = np.ascontiguousarray(np.asarray(inputs['w_pw2'], np.float32)[0])
    w_out = np.ascontiguousarray(np.asarray(inputs['w_out'], np.float32)[0])
    wvec = np.empty((128, 2 * D), np.float32)
    wvec[:, 0:D] = np.asarray(inputs['norm_pre_w'], np.float32)[0][None, :]
    wvec[:, D:] = np.asarray(inputs['norm_post_w'], np.float32)[0][None, :]
    npin = {k: np.asarray(v, np.float32) for k, v in inputs.items()}
    maps = []
    for core in range(8):
        b, half = core // 2, core % 2
        x_all = np.zeros((2048, D), np.float32)
        if half == 1:
            x_all[0:T] = x[b, 0:T]
        x_all[T:] = x[b, half * T:(half + 1) * T]
        cf, cb = host_consts(npin, half)
        maps.append({"x_all": x_all, "w_in": w_in, "w_o": w_o, "w_pw2": w_pw2, "w_out": w_out,
                     "cf": cf, "cb": cb, "wvec": wvec})
    return maps


def kernel(**inputs):
    nc = build()
    maps = make_in_maps(inputs)
    res = run_bass_kernel_spmd(nc, maps, core_ids=list(range(8)))
    out = np.empty((NB, S, D), np.float32)
    for core in range(8):
        b, half = core // 2, core % 2
        out[b, half * T:(half + 1) * T] = np.asarray(res.results[core]["out"], np.float32)
    return out
```

```python
import math
from contextlib import ExitStack

import numpy as np
import ml_dtypes

import concourse.bass as bass
import concourse.mybir as mybir
from concourse.bass_utils import run_bass_kernel_spmd

F32 = mybir.dt.float32
BF16 = mybir.dt.bfloat16
AF = mybir.ActivationFunctionType
ALU = mybir.AluOpType

D = 2048
S = 2048
NB = 4
T = 1024
DIN = 11264
C_Q, C_K, C_V, C_ZA, C_GLU, C_ZC, C_GA, C_GC = 0, 1024, 2048, 3072, 4096, 6144, 7168, 9216
LAM_INIT = 0.8 - 0.6 * math.exp(-0.3 * 0)
CONVW = 31
ROPE_THETA = 500000.0

CF_COS, CF_SIN, CF_BGLU, CF_BDW, CF_LNG, CF_LNB, CF_BPW2, CF_WDW, CF_LAM, CF_SUBLN, CF_HALO, CF_N = \
    0, 128, 256, 272, 280, 288, 296, 312, 560, 816, 944, 945
CB_ID, CB_TRI, CB_VONES, CB_N = 0, 128, 256, 272


class Trk:
    ENG = ('pe', 'act', 'dve', 'pool', 'sp')

    def __init__(self, nc, es):
        self.nc = nc
        self.es = es
        self.sem = {k: es.enter_context(nc.semaphore("s_" + k)) for k in self.ENG}
        self.cnt = {k: 0 for k in self.ENG}
        self.seen = {k: {} for k in self.ENG}
        self.prog = {k: [] for k in self.ENG}
        self.lastw = {}
        self.readers = {}
        self.dsem = {}
        self.dcnt = {}
        self.pe_dirty = False

    def _wait(self, e, key, val):
        if self.seen[e].get(key, 0) >= val:
            return
        if key == e and val > self.cnt[e]:
            return
        self.seen[e][key] = val
        self.prog[e].append(('wait', key, val))

    def _deps(self, e, reads, writes):
        best = {}
        for r in reads:
            tk = self.lastw.get(r)
            if tk is not None:
                best[tk[0]] = max(best.get(tk[0], 0), tk[1])
        for w in writes:
            tk = self.lastw.get(w)
            if tk is not None:
                best[tk[0]] = max(best.get(tk[0], 0), tk[1])
            for k, v in self.readers.get(w, ()):
                best[k] = max(best.get(k, 0), v)
        for k, v in best.items():
            self._wait(e, k, v)

    def _record(self, tok, reads, writes):
        for r in reads:
            lst = self.readers.setdefault(r, [])
            for i, (k, v) in enumerate(lst):
                if k == tok[0]:
                    lst[i] = (k, max(v, tok[1]))
                    break
            else:
                lst.append(tok)
        for w in writes:
            self.lastw[w] = tok
            self.readers[w] = []

    def op(self, e, fn, reads=(), writes=(), sig=True):
        self._deps(e, reads, writes)
        if sig:
            self.cnt[e] += 1
            tok = (e, self.cnt[e])
            if e == 'pe':
                self.pe_dirty = False
        else:
            tok = (e, self.cnt[e] + 1)
            if e == 'pe':
                self.pe_dirty = True
        self.prog[e].append(('op', fn, sig))
        self._record(tok, reads, writes)
        return tok

    def dma(self, e, out, in_, reads=(), writes=(), key=None):
        assert key is not None
        if key not in self.dsem:
            self.dsem[key] = self.es.enter_context(self.nc.semaphore("d%d" % len(self.dsem)))
            self.dcnt[key] = 0
        self._deps(e, reads, writes)
        self.dcnt[key] += 16
        self.prog[e].append(('dma', (out, in_), key))
        tok = (key, self.dcnt[key])
        self._record(tok, reads, writes)
        return tok

    def barrier(self, engines=('pe', 'act', 'dve', 'pool', 'sp')):
        assert not self.pe_dirty
        for e in engines:
            for o in engines:
                if o != e and self.cnt[o] > 0:
                    self._wait(e, o, self.cnt[o])
            for k in list(self.dcnt):
                if isinstance(k, tuple) and k[0] == 'dbg':
                    self._wait(e, k, self.dcnt[k])

    def wait_dbg(self, engines=('pe', 'act', 'dve', 'pool', 'sp')):
        for e in engines:
            for k in list(self.dcnt):
                if isinstance(k, tuple) and k[0] == 'dbg':
                    self._wait(e, k, self.dcnt[k])

    def wait_all_dma(self, e, keys):
        for k in keys:
            self._wait(e, k, self.dcnt[k])

    def check(self):
        val = {}
        ptr = {e: 0 for e in self.ENG}
        progress = True
        while progress:
            progress = False
            for e in self.ENG:
                p = self.prog[e]
                while ptr[e] < len(p):
                    it = p[ptr[e]]
                    if it[0] == 'wait':
                        if val.get(it[1], 0) < it[2]:
                            break
                    elif it[0] == 'op':
                        if it[2]:
                            val[e] = val.get(e, 0) + 1
                    else:
                        val[it[2]] = val.get(it[2], 0) + 16
                    ptr[e] += 1
                    progress = True
        for e in self.ENG:
            if ptr[e] < len(self.prog[e]):
                raise RuntimeError("DEADLOCK: engine %s stuck at %d/%d: %r" % (
                    e, ptr[e], len(self.prog[e]), self.prog[e][ptr[e]][:3]))

    def emit(self):
        self.check()
        nc = self.nc

        def run(e, eng):
            sem = self.sem[e]
            for it in self.prog[e]:
                if it[0] == 'wait':
                    k = it[1]
                    eng.wait_ge(self.sem[k] if k in self.sem else self.dsem[k], it[2])
                elif it[0] == 'op':
                    ins = it[1](eng)
                    if it[2]:
                        ins.then_inc(sem, 1)
                else:
                    eng.dma_start(out=it[1][0], in_=it[1][1]).then_inc(self.dsem[it[2]], 16)

        with nc.Block() as block:
            @block.tensor
            def _(eng):
                run('pe', eng)

            @block.scalar
            def _(eng):
                run('act', eng)

            @block.vector
            def _(eng):
                run('dve', eng)

            @block.gpsimd
            def _(eng):
                run('pool', eng)

            @block.sync
            def _(eng):
                run('sp', eng)


class Bump:
    def __init__(self, ap, ncols):
        self.ap = ap
        self.n = ncols
        self.off = 0

    def reset(self):
        self.off = 0

    def bf(self, n):
        assert self.off + n <= self.n, (self.off, n, self.n)
        a = self.ap[:, self.off:self.off + n]
        self.off += n
        return a

    def f32(self, n):
        if self.off % 2:
            self.off += 1
        return self.bf(2 * n).bitcast(F32)


WARM_JUNK = 0
FIN_DELAY = 2
ARENA_COLS = 37376


def build(debug=False, stop_after=None):
    nc = bass.Bass("TRN2", target_bir_lowering=False)

    def dram(name, shape, dt, kind="ExternalInput"):
        return nc.dram_tensor(name, shape, dt, kind=kind).ap()

    x_all = dram("x_all", [2048, D], F32)
    w_in = dram("w_in", [D, DIN], F32)
    w_o = dram("w_o", [1024, D], F32)
    w_pw2 = dram("w_pw2", [1024, D], F32)
    w_out = dram("w_out", [D, D], F32)
    cf_d = dram("cf", [128, CF_N], F32)
    cb_d = dram("cb", [128, CB_N], BF16)
    wvec_d = dram("wvec", [128, 2 * D], F32)
    out_d = dram("out", [T, D], F32, kind="ExternalOutput")
    dbg = {}
    if debug:
        for nm, shp, dt in (("d_hT", [128, 16 * 2048], BF16), ("d_uT", [128, 8 * 1056], BF16),
                            ("d_u2T", [128, 8 * 1024], BF16), ("d_ozT", [128, 8 * 1024], BF16),
                            ("d_mT", [128, 16 * 1024], BF16), ("d_zcT", [128, 8 * 1024], BF16),
                            ("d_KT", [128, 4 * 2048], BF16), ("d_Q0", [128, 4 * 1024], BF16),
                            ("d_vaug", [128, 16 * 4 * 130], BF16), ("d_zsb", [128, 8 * 512], BF16)):
            dbg[nm] = dram(nm, shp, dt, kind="ExternalOutput")

    w_in_v = w_in.rearrange("(a p) n -> p a n", p=128)
    w_o_v = w_o.rearrange("(a p) n -> p a n", p=128)
    w_pw2_v = w_pw2.rearrange("(a p) n -> p a n", p=128)
    w_out_v = w_out.rearrange("(a p) n -> p a n", p=128)

    es = ExitStack()
    with es:
        t = Trk(nc, es)
        hT_t = es.enter_context(nc.sbuf_tensor("hT", [128, 16 * 2048], BF16))
        ozT_t = es.enter_context(nc.sbuf_tensor("ozT", [128, 8 * 1024], BF16))
        u2T_t = es.enter_context(nc.sbuf_tensor("u2T", [128, 8 * 1024], BF16))
        wr_t = es.enter_context(nc.sbuf_tensor("wring", [128, 16384], BF16))
        cf_t = es.enter_context(nc.sbuf_tensor("cfs", [128, CF_N + 3], F32))
        cb_t = es.enter_context(nc.sbuf_tensor("cbs", [128, CB_N], BF16))
        sm_t = es.enter_context(nc.sbuf_tensor("small", [128, 640], F32))
        arena_t = es.enter_context(nc.sbuf_tensor("arena", [128, ARENA_COLS], BF16))
        ps_t = es.enter_context(nc.psum_tensor("ps", [128, 4096], F32))

        hT = hT_t[:, :].rearrange("p (a n) -> p a n", a=16)
        ozT = ozT_t[:, :].rearrange("p (a n) -> p a n", a=8)
        u2T = u2T_t[:, :].rearrange("p (a n) -> p a n", a=8)
        wring = wr_t[:, :]
        cfs = cf_t[:, :]
        cbs = cb_t[:, :]
        small = sm_t[:, :]
        ps = ps_t[:, :]
        A = Bump(arena_t[:, :], ARENA_COLS)

        def finish():
            t.barrier()
            t.wait_all_dma('sp', list(t.dsem.keys()))
            print("instr counts:", {e: len(t.prog[e]) for e in t.ENG}, "sems:", len(t.dsem) + 5)
            t.emit()
            return nc

        ident = cbs[:, CB_ID:CB_ID + 128]
        tri = cbs[:, CB_TRI:CB_TRI + 128]
        vones = cbs[:, CB_VONES:CB_VONES + 16]
        cosv = cfs[:, CF_COS:CF_COS + 128].rearrange("p (t f) -> p t f", t=16)
        sinv = cfs[:, CF_SIN:CF_SIN + 128].rearrange("p (t f) -> p t f", t=16)
        bglu = cfs[:, CF_BGLU:CF_BGLU + 16]
        bdw = cfs[:, CF_BDW:CF_BDW + 8]
        lng = cfs[:, CF_LNG:CF_LNG + 8]
        lnb = cfs[:, CF_LNB:CF_LNB + 8]
        bpw2 = cfs[:, CF_BPW2:CF_BPW2 + 16]
        wdw = cfs[:, CF_WDW:CF_WDW + 248].rearrange("p (c j) -> p c j", c=8)
        lamv = cfs[:, CF_LAM:CF_LAM + 256]
        sublnw = cfs[:, CF_SUBLN:CF_SUBLN + 128]
        halo = cfs[:, CF_HALO:CF_HALO + 1]
        eps6 = cfs[:, CF_N:CF_N + 1]
        eps5 = cfs[:, CF_N + 1:CF_N + 2]
        neglam = cfs[:, CF_N + 2:CF_N + 3]

        ssq = small[:, 0:16]
        sqv = small[:, 16:32]
        rstd = small[:, 32:48]
        lamt = small[:, 48:56]
        ssq2 = small[:, 56:64]
        sq2 = small[:, 64:72]
        rstd2 = small[:, 72:80]
        fin = small[:, 128:512].rearrange("p (k s) -> p k s", s=6)

        def bank(b, n=512, off=0):
            return ps[:, 512 * b + off:512 * b + off + n]

        def bank_bf(b):
            return ps[:, 512 * b:512 * b + 512].bitcast(BF16)

        def load_w(col0, ncols, src3, wkey):
            a_tot, n = src3.shape[1], src3.shape[2]
            assert a_tot * n == ncols
            dst = wring[:, col0:col0 + ncols].rearrange("p (a n) -> p a n", a=a_tot)
            step = max(1, a_tot // 4)
            res_all = []
            for a0 in range(0, a_tot, step):
                c_lo = col0 + a0 * n
                c_hi = col0 + (a0 + step) * n
                res = [('W', q) for q in range(c_lo // 1024, (c_hi - 1) // 1024 + 1)]
                tok = t.dma('pool', dst[:, a0:a0 + step, :], src3[:, a0:a0 + step, :], writes=res, key=wkey)
                res_all += res
            for r in res_all:
                t.lastw[r] = tok
            return sorted(set(res_all))

        t.dma('pool', cfs[:, 0:CF_N], cf_d, writes=['cf'], key='cf')
        t.dma('pool', cbs, cb_d, writes=['cb'], key='cb')
        t.op('dve', lambda e: e.memset(small, 0.0), writes=['small'])
        t.op('dve', lambda e: e.memset(eps6, 1e-6), writes=['eps6'])
        t.op('dve', lambda e: e.memset(eps5, 1e-5), writes=['eps5'])
        lj = small[:, 512:640]
        t.op('dve', lambda e: e.tensor_tensor(out=lj[:, 0:64], in0=lamv[:, 0:64], in1=lamv[:, 64:128], op=ALU.mult),
             reads=['cf', 'small'], writes=['lj0'])
        t.op('dve', lambda e: e.tensor_tensor(out=lj[:, 64:128], in0=lamv[:, 128:192], in1=lamv[:, 192:256], op=ALU.mult),
             reads=['cf', 'small'], writes=['lj1'])
        t.op('dve', lambda e: e.tensor_reduce(out=lamt[:, 0:1], in_=lj[:, 0:64], axis=mybir.AxisListType.X, op=ALU.add),
             reads=['lj0', 'small'], writes=['lam0'])
        t.op('dve', lambda e: e.tensor_reduce(out=lamt[:, 1:2], in_=lj[:, 64:128], axis=mybir.AxisListType.X, op=ALU.add),
             reads=['lj1', 'small'], writes=['lam1'])
        t.op('act', lambda e: e.activation(out=lamt[:, 2:4], in_=lamt[:, 0:2], func=AF.Exp), reads=['lam0', 'lam1'], writes=['lam2'])
        t.op('dve', lambda e: e.scalar_tensor_tensor(out=neglam, in0=lamt[:, 3:4], scalar=-LAM_INIT, in1=lamt[:, 2:3],
                                                     op0=ALU.add, op1=ALU.subtract), reads=['lam2'], writes=['neglam'])
        t.op('dve', lambda e: e.tensor_scalar(out=sublnw, in0=sublnw, scalar1=1.0 - LAM_INIT, scalar2=None, op0=ALU.mult),
             reads=['cf'], writes=['sublnw'])

        A.reset()
        xs = [A.f32(D) for _ in range(4)]
        htm = [A.bf(D), A.bf(D)]
        junk = A.bf(D)
        wvsA = A.f32(D)
        t.dma('pool', wvsA, wvec_d[:, 0:D], writes=['wv'], key='wv')
        a_cnt = [0]

        def A_block(tb):
            p = a_cnt[0] % 2
            px = a_cnt[0] % 4
            a_cnt[0] += 1
            t.dma('sp', xs[px], x_all[tb * 128:(tb + 1) * 128, :], writes=[('xs', px)], key=('xs', px))
            t.op('act', lambda e, px=px, tb=tb: e.activation(out=junk, in_=xs[px], func=AF.Square, accum_out=ssq[:, tb:tb + 1]),
                 reads=[('xs', px), 'small'], writes=['junk', ('ssq', tb)])
            t.op('act', lambda e, tb=tb: e.activation(out=sqv[:, tb:tb + 1], in_=ssq[:, tb:tb + 1], func=AF.Sqrt,
                                                      scale=1.0 / D, bias=eps6), reads=[('ssq', tb), 'eps6'], writes=[('sqv', tb)])
            t.op('dve', lambda e, tb=tb: e.reciprocal(out=rstd[:, tb:tb + 1], in_=sqv[:, tb:tb + 1]),
                 reads=[('sqv', tb)], writes=[('rstd', tb)])
            t.op('dve', lambda e, p=p, px=px, tb=tb: e.scalar_tensor_tensor(out=htm[p], in0=xs[px], scalar=rstd[:, tb:tb + 1], in1=wvsA,
                                                                     op0=ALU.mult, op1=ALU.mult),
                 reads=[('xs', px), ('rstd', tb), 'wv'], writes=[('htm', p)])
            return p

        def A_block2(tb, p):
            b0 = 2 * p
            for dc in range(16):
                b = b0 + dc // 8
                o = bank_bf(b)[:, (dc % 8) * 128:(dc % 8) * 128 + 128]
                t.op('pe', lambda e, o=o, p=p, dc=dc: e.transpose(out=o, in_=htm[p][:, dc * 128:(dc + 1) * 128], identity=ident),
                     reads=[('htm', p), 'cb'], writes=[('ps', b)], sig=(dc % 8 == 7))
            src0 = bank_bf(b0).rearrange("p (a n) -> p a n", a=8)
            src1 = bank_bf(b0 + 1).rearrange("p (a n) -> p a n", a=8)
            t.op('act', lambda e, src0=src0, tb=tb: e.activation(out=hT[:, 0:8, tb * 128:(tb + 1) * 128], in_=src0, func=AF.Copy),
                 reads=[('ps', b0)], writes=[('hT', tb, 0)])
            t.op('dve', lambda e, src1=src1, tb=tb: e.tensor_copy(out=hT[:, 8:16, tb * 128:(tb + 1) * 128], in_=src1),
                 reads=[('ps', b0 + 1)], writes=[('hT', tb, 1)])
        hT_all = [('hT', tb, k) for tb in range(16) for k in range(2)]
        hT_own = [('hT', tb, k) for tb in range(8, 16) for k in range(2)]

        if stop_after == 'A':
            return finish()
        def e1_load(cp):
            u = (cp % 2) * 2
            ra = load_w(u * 4096, 4096, w_in_v[:, :, C_GLU + cp * 256:C_GLU + cp * 256 + 256], ('Wk', u))
            rg = load_w((u + 1) * 4096, 4096, w_in_v[:, :, C_GLU + 1024 + cp * 256:C_GLU + 1024 + cp * 256 + 256], ('Wk', u + 1))
            return u, ra, rg
        pend = e1_load(0)
        prevA = None
        for tb in range(16):
            p_ = A_block(tb)
            if prevA is not None:
                A_block2(*prevA)
            prevA = (tb, p_)
        A_block2(*prevA)
        a_rest = []
        t.barrier()
        A.reset()
        uT = A.bf(8 * 1056).rearrange("p (a n) -> p a n", a=8)
        zcT = A.bf(8 * 1024).rearrange("p (a n) -> p a n", a=8)
        e3_off = A.off
        sgs = [A.f32(352), A.f32(352)]
        it = 0
        for cp in range(4):
            u, ra, rg = pend
            if cp + 1 < 4:
                pend = e1_load(cp + 1)
            wa = wring[:, u * 4096:(u + 1) * 4096].rearrange("p (a n) -> p a n", a=16)
            wg = wring[:, (u + 1) * 4096:(u + 2) * 4096].rearrange("p (a n) -> p a n", a=16)
            for ci in range(2):
                cc = cp * 2 + ci
                for g in range(3):
                    c0 = 992 + 352 * g
                    hres = [('hT', tb_, k_) for tb_ in range(c0 // 128, (c0 + 351) // 128 + 1) for k_ in range(2)]
                    par = it % 2
                    it += 1
                    bA, bG = 2 * par, 2 * par + 1
                    for dc in range(16):
                        t.op('pe', lambda e, dc=dc, bA=bA, c0=c0, wa=wa, ci=ci: e.matmul(
                            bank(bA, 352), lhsT=wa[:, dc, ci * 128:(ci + 1) * 128], rhs=hT[:, dc, c0:c0 + 352],
                            start=(dc == 0), stop=(dc == 15)), reads=ra + hres, writes=[('ps', bA)], sig=(dc == 15))
                    for dc in range(16):
                        t.op('pe', lambda e, dc=dc, bG=bG, c0=c0, wg=wg, ci=ci: e.matmul(
                            bank(bG, 352), lhsT=wg[:, dc, ci * 128:(ci + 1) * 128], rhs=hT[:, dc, c0:c0 + 352],
                            start=(dc == 0), stop=(dc == 15)), reads=rg + hres, writes=[('ps', bG)], sig=(dc == 15))
                    t.op('act', lambda e, par=par, bG=bG, cc=cc: e.activation(out=sgs[par], in_=bank(bG, 352), func=AF.Sigmoid,
                                                                              bias=bglu[:, 8 + cc:9 + cc]),
                         reads=[('ps', bG), 'cf'], writes=[('sgs', par)])
                    t.op('dve', lambda e, par=par, bA=bA, cc=cc, g=g: e.scalar_tensor_tensor(
                        out=uT[:, cc, 352 * g:352 * g + 352], in0=bank(bA, 352), scalar=bglu[:, cc:cc + 1], in1=sgs[par],
                        op0=ALU.add, op1=ALU.mult), reads=[('ps', bA), ('sgs', par), 'cf'], writes=[('uT', cc, g)])
                    if a_rest:
                        A_block(a_rest.pop(0))
        uT_all = [('uT', cc, g) for cc in range(8) for g in range(3)]
        t.op('dve', lambda e: e.tensor_scalar(out=uT[:, :, 0:32], in0=uT[:, :, 0:32], scalar1=halo, scalar2=None, op0=ALU.mult),
             reads=[('uT', cc, 0) for cc in range(8)] + ['cf'], writes=[('uT', cc, 0) for cc in range(8)])

        if stop_after == 'E1':
            return finish()
        def e2_load(cq):
            u = (cq % 2) * 2
            r = load_w(u * 4096, 8192, w_in_v[:, :, C_ZC + cq * 512:C_ZC + cq * 512 + 512], ('Wk', u))
            return u, r
        pend = e2_load(0)
        for cq in range(2):
            u, rw = pend
            if cq + 1 < 2:
                pend = e2_load(cq + 1)
            wz = wring[:, u * 4096:(u + 2) * 4096].rearrange("p (a n) -> p a n", a=16)
            for ci in range(4):
                cc = cq * 4 + ci
                for tg in range(2):
                    par = it % 2
                    it += 1
                    b = par
                    for dc in range(16):
                        t.op('pe', lambda e, dc=dc, b=b, wz=wz, ci=ci, tg=tg: e.matmul(
                            bank(b), lhsT=wz[:, dc, ci * 128:(ci + 1) * 128], rhs=hT[:, dc, 1024 + tg * 512:1024 + tg * 512 + 512],
                            start=(dc == 0), stop=(dc == 15)), reads=rw + hT_own, writes=[('ps', b)], sig=(dc == 15))
                    t.op('act', lambda e, b=b, cc=cc, tg=tg: e.activation(out=zcT[:, cc, tg * 512:(tg + 1) * 512], in_=bank(b), func=AF.Silu),
                         reads=[('ps', b)], writes=[('zcT', cc, tg)])

        if stop_after == 'E2':
            return finish()
        assert not a_rest
        A.off = e3_off
        yT = A.f32(8 * 512).rearrange("p (a n) -> p a n", a=8)
        ysq = [A.f32(512), A.f32(512)]
        mean_b = A.f32(512)
        var_b = A.f32(512)
        diag = [A.bf(31 * 128).rearrange("p (j n) -> p j n", j=31), A.bf(31 * 128).rearrange("p (j n) -> p j n", j=31)]
        onesf = A.f32(128)
        t.op('dve', lambda e: e.memset(onesf, 1.0 / 1024), writes=['onesf'])
        def diag_build(cc):
            par = cc % 2
            dg = diag[par]
            t.op('dve', lambda e: e.tensor_tensor(
                out=dg, in0=ident.unsqueeze(1).to_broadcast([128, 31, 128]),
                in1=wdw[:, cc, :].unsqueeze(2).to_broadcast([128, 31, 128]), op=ALU.mult),
                reads=['cb', 'cf'], writes=[('diag', par)])

        def conv_mm(th, cc):
            t0 = 512 * th
            par = cc % 2
            dg = diag[par]
            nxt = cc + 1 if cc < 7 else (0 if th == 0 else None)
            if nxt is not None:
                pass
            for j in range(CONVW):
                t.op('pe', lambda e, j=j: e.matmul(
                    bank(par), lhsT=dg[:, j, :], rhs=uT[:, cc, t0 + 2 + j:t0 + 2 + j + 512], start=(j == 0), stop=(j == CONVW - 1)),
                    reads=[('diag', par)] + [('uT', cc, g) for g in range(3)], writes=[('ps', par)], sig=(j == CONVW - 1))

        def evac_act(th, cc):
            par = cc % 2
            t.op('act', lambda e: e.activation(out=yT[:, cc, :], in_=bank(par), func=AF.Identity, bias=bdw[:, cc:cc + 1]),
                 reads=[('ps', par), 'cf'], writes=[('yT', cc)] + ([('sgs', 0), ('sgs', 1)] if cc < 2 else []))
            t.op('act', lambda e: e.activation(out=ysq[par], in_=yT[:, cc, :], func=AF.Square),
                 reads=[('yT', cc)], writes=[('ysq', par)])

        def stats_mm(th, cc):
            par = cc % 2
            t.op('pe', lambda e: e.matmul(bank(2), lhsT=onesf, rhs=yT[:, cc, :], start=(cc == 0), stop=(cc == 7)),
                 reads=['onesf', ('yT', cc)], writes=[('ps', 2)], sig=True)
            t.op('pe', lambda e: e.matmul(bank(3), lhsT=onesf, rhs=ysq[par], start=(cc == 0), stop=(cc == 7)),
                 reads=['onesf', ('ysq', par)], writes=[('ps', 3)], sig=True)

        def stats_finish():
            t.op('act', lambda e: e.activation(out=mean_b, in_=bank(2), func=AF.Copy), reads=[('ps', 2)], writes=['mean_b'])
            t.op('dve', lambda e: e.tensor_tensor(out=var_b, in0=mean_b, in1=mean_b, op=ALU.mult), reads=['mean_b'], writes=['var_b'])
            t.op('dve', lambda e: e.tensor_tensor(out=var_b, in0=bank(3), in1=var_b, op=ALU.subtract), reads=[('ps', 3), 'var_b'], writes=['var_b'])
            t.op('act', lambda e: e.activation(out=var_b, in_=var_b, func=AF.Sqrt, bias=eps5), reads=['var_b', 'eps5'], writes=['var_b'])
            t.op('dve', lambda e: e.reciprocal(out=var_b, in_=var_b), reads=['var_b'], writes=['var_b'])

        def epi_a(th, cc):
            t.op('dve', lambda e: e.tensor_tensor(out=yT[:, cc, :], in0=yT[:, cc, :], in1=mean_b, op=ALU.subtract),
                 reads=[('yT', cc), 'mean_b'], writes=[('yT', cc)])
            t.op('dve', lambda e: e.tensor_tensor(out=yT[:, cc, :], in0=yT[:, cc, :], in1=var_b, op=ALU.mult),
                 reads=[('yT', cc), 'var_b'], writes=[('yT', cc)])
            t.op('act', lambda e: e.activation(out=yT[:, cc, :], in_=yT[:, cc, :], func=AF.Silu,
                                               scale=lng[:, cc:cc + 1], bias=lnb[:, cc:cc + 1]),
                 reads=[('yT', cc), 'cf'], writes=[('yT', cc)])

        def epi_b(th, cc):
            t0 = 512 * th
            t.op('dve', lambda e: e.tensor_tensor(out=u2T[:, cc, t0:t0 + 512], in0=yT[:, cc, :],
                                                  in1=zcT[:, cc, t0:t0 + 512], op=ALU.mult),
                 reads=[('yT', cc), ('zcT', cc, th)], writes=[('u2T', cc, th)])

        diag_build(0)
        for cc in range(8):
            diag_build((cc + 1) % 8)
            conv_mm(0, cc)
            if cc > 0:
                stats_mm(0, cc - 1)
            evac_act(0, cc)
        stats_mm(0, 7)
        stats_finish()
        for cc in range(8):
            if cc < 7:
                diag_build(cc + 1)
            conv_mm(1, cc)
            if cc > 0:
                stats_mm(1, cc - 1)
            epi_a(0, cc)
            epi_b(0, cc)
            evac_act(1, cc)
        stats_mm(1, 7)
        stats_finish()
        epi_queue = []
        for cc in range(9):
            if cc < 8:
                epi_queue.append(lambda cc=cc: epi_a(1, cc))
            if cc > 0:
                epi_queue.append(lambda cc=cc: epi_b(1, cc - 1))
        if debug:
            while epi_queue:
                epi_queue.pop(0)()
        if debug:
            t.dma('sp', dbg["d_hT"], hT_t[:, :], reads=hT_all, key=('dbg', 1))
            t.dma('sp', dbg["d_uT"], uT.rearrange("p a n -> p (a n)"), reads=uT_all, key=('dbg', 2))
            t.dma('sp', dbg["d_zcT"], zcT.rearrange("p a n -> p (a n)"), reads=[('zcT', cc, tg) for cc in range(8) for tg in range(2)], key=('dbg', 3))
            t.dma('sp', dbg["d_u2T"], u2T_t[:, :], reads=[('u2T', cc, th) for cc in range(8) for th in range(2)], key=('dbg', 4))

        if stop_after == 'E3':
            return finish()
        def bcd_load(hg, which):
            col = {'k': C_K, 'v': C_V, 'q': C_Q, 'z': C_ZA}[which] + hg * 512
            u = {'k': 0, 'v': 2, 'q': 0, 'z': 2}[which]
            r = load_w(u * 4096, 8192, w_in_v[:, :, col:col + 512], ('Wk', u))
            return u, r

        fkc = [0]
        for hg in range(2):
            pk = bcd_load(hg, 'k')
            pv = bcd_load(hg, 'v')
            ozT4 = ozT_t[:, :].rearrange("p (i j n) -> p i j n", i=8, j=4)
            zcol = C_ZA + hg * 512
            za_res = [('oz', i_, h_) for i_ in range(8) for h_ in range(4, 8)]
            for i2 in range(8):
                tokz = t.dma('pool', ozT4[:, i2, 2:4, :], w_in_v[:, 2 * i2:2 * i2 + 2, zcol:zcol + 256],
                             writes=[('oz', i2, h_) for h_ in range(4, 8)], key=('za', hg))
            for r_ in za_res:
                t.lastw[r_] = tokz
            if hg == 0:
                zb_res = [('oz', i_, h_) for i_ in range(8) for h_ in range(0, 4)]
                for i2 in range(8):
                    tokz = t.dma('pool', ozT4[:, i2, 0:2, :], w_in_v[:, 2 * i2:2 * i2 + 2, zcol + 256:zcol + 512],
                                 writes=[('oz', i2, h_) for h_ in range(0, 4)], key=('zb', hg))
                for r_ in zb_res:
                    t.lastw[r_] = tokz
            if debug:
                t.wait_dbg()
            A.reset()
            KT = A.bf(4 * 2048).rearrange("p (a n) -> p a n", a=4)
            QT = A.bf(4 * 1024).rearrange("p (a n) -> p a n", a=4)
            vaug = A.bf(16 * 4 * 130).rearrange("p (m h e) -> p m h e", m=16, h=4)
            zsb = A.bf(8 * 512).rearrange("p (a n) -> p a n", a=8)
            ET = [A.bf(1024).rearrange("p (c n) -> p c n", c=2) for _ in range(3)]
            t0s = [A.f32(128) for _ in range(4)]
            osb = [A.f32(128) for _ in range(4)]
            ztm = [A.f32(512), A.f32(512)]
            ktm = [A.bf(512), A.bf(512)]
            rt = [A.f32(64).rearrange("p (g d) -> p g d", g=8) for _ in range(4)]
            assert A.off - 1536 >= 28928 + 512, A.off

            def bcd_setup():
                t.op('dve', lambda e: e.tensor_copy(out=vaug[:, :, :, 128:129], in_=vones.unsqueeze(2).unsqueeze(3).to_broadcast([128, 16, 4, 1])),
                     reads=['cb'], writes=['vones'])
                t.op('dve', lambda e: e.memset(vaug[:, :, :, 129:130], 0.0), writes=['vpad'])
            if hg == 1:
                bcd_setup()

            def proj(tb, wpanel, rw, b):
                for dc in range(16):
                    t.op('pe', lambda e, dc=dc, tb=tb, wpanel=wpanel, b=b: e.matmul(
                        bank(b), lhsT=hT[:, dc, tb * 128:(tb + 1) * 128], rhs=wpanel[:, dc, :], start=(dc == 0), stop=(dc == 15)),
                        reads=rw + [('hT', tb, 0), ('hT', tb, 1)], writes=[('ps', b)], sig=(dc == 15))

            def rotary(b, tb, kt, kres):
                src = bank(b).rearrange("p (g d) -> p g d", g=8)
                dst = kt.rearrange("p (g d) -> p g d", g=8)
                cb_ = cosv[:, tb, :].unsqueeze(1).to_broadcast([128, 8, 8])
                sb_ = sinv[:, tb, :].unsqueeze(1).to_broadcast([128, 8, 8])
                t.op('act', lambda e: e.activation(out=kt, in_=bank(b), func=AF.Copy), reads=[('ps', b)], writes=[kres])
                x1 = src[:, :, 0:8]
                x2 = src[:, :, 8:16]
                t.op('dve', lambda e: e.tensor_tensor(out=rt[0], in0=x1, in1=cb_, op=ALU.mult), reads=[('ps', b), 'cf', kres], writes=['rt0'])
                t.op('dve', lambda e: e.tensor_tensor(out=rt[1], in0=x2, in1=sb_, op=ALU.mult), reads=[('ps', b), 'cf', kres], writes=['rt1'])
                t.op('dve', lambda e: e.tensor_tensor(out=rt[2], in0=x2, in1=cb_, op=ALU.mult), reads=[('ps', b), 'cf', kres], writes=['rt2'])
                t.op('dve', lambda e: e.tensor_tensor(out=rt[3], in0=x1, in1=sb_, op=ALU.mult), reads=[('ps', b), 'cf', kres], writes=['rt3'])
                t.op('dve', lambda e: e.tensor_tensor(out=dst[:, :, 0:8], in0=rt[0], in1=rt[1], op=ALU.subtract),
                     reads=['rt0', 'rt1'], writes=[kres])
                t.op('dve', lambda e: e.tensor_tensor(out=dst[:, :, 8:16], in0=rt[2], in1=rt[3], op=ALU.add),
                     reads=['rt2', 'rt3'], writes=[kres])

            u, rw = pk
            wk = wring[:, u * 4096:(u + 2) * 4096].rearrange("p (a n) -> p a n", a=16)

            def k_stage1(tb):
                par = tb % 2
                proj(tb, wk, rw, par)
                rotary(par, tb, ktm[par], ('ktm', par))

            def k_stage2(tb):
                par = tb % 2
                tbk = 2 + par
                for hl in range(4):
                    o = bank_bf(tbk)[:, hl * 128:(hl + 1) * 128]
                    t.op('pe', lambda e, o=o, par=par, hl=hl: e.transpose(out=o, in_=ktm[par][:, hl * 128:(hl + 1) * 128], identity=ident),
                         reads=[('ktm', par), 'cb'], writes=[('ps', tbk)], sig=(hl == 3))
                srcT = bank_bf(tbk)[:, 0:512].rearrange("p (a n) -> p a n", a=4)
                t.op('dve' if par else 'act',
                     (lambda e, srcT=srcT, tb=tb: e.tensor_copy(out=KT[:, :, tb * 128:(tb + 1) * 128], in_=srcT)) if par else
                     (lambda e, srcT=srcT, tb=tb: e.activation(out=KT[:, :, tb * 128:(tb + 1) * 128], in_=srcT, func=AF.Copy)),
                     reads=[('ps', tbk)], writes=[('KT', tb)])
            for tb in range(17):
                for _ in range(3):
                    if epi_queue:
                        epi_queue.pop(0)()
                if tb < 16:
                    k_stage1(tb)
                if tb > 0:
                    k_stage2(tb - 1)
            assert not epi_queue
            if hg == 0:
                t.barrier()
                bcd_setup()
            if stop_after == 'B1':
                return finish()
            pq = bcd_load(hg, 'q')
            u, rw = pv
            wvp = wring[:, u * 4096:(u + 2) * 4096].rearrange("p (a n) -> p a n", a=16)
            for tb in range(16):
                par = tb % 2
                proj(tb, wvp, rw, par)
                srcV = bank(par).rearrange("p (h e) -> p h e", h=4)
                t.op('dve' if par else 'act',
                     (lambda e, srcV=srcV, tb=tb: e.tensor_copy(out=vaug[:, tb, :, 0:128], in_=srcV)) if par else
                     (lambda e, srcV=srcV, tb=tb: e.activation(out=vaug[:, tb, :, 0:128], in_=srcV, func=AF.Copy)),
                     reads=[('ps', par)], writes=[('vaug', tb)])
            if stop_after == 'B2':
                return finish()
            if hg == 1:
                zb_res = load_w(2 * 4096, 4096, w_in_v[:, :, zcol + 256:zcol + 512], ('Wk', 2))
            if hg == 1:
                woutA = hT[:, :, 0:1024]
                prev_res = [('hT', tb_, k_) for tb_ in range(8) for k_ in range(2)]
                for q4 in range(4):
                    tokA = t.dma('pool', woutA[:, 4 * q4:4 * q4 + 4, :], w_out_v[:, 4 * q4:4 * q4 + 4, 0:1024],
                                 writes=[('woutA', q4)] + prev_res, key='woutA')
                for q4 in range(4):
                    t.lastw[('woutA', q4)] = tokA
            u, rw = pq
            wq = wring[:, u * 4096:(u + 2) * 4096].rearrange("p (a n) -> p a n", a=16)

            def q_stage1(i):
                par = i % 2
                proj(8 + i, wq, rw, par)
                rotary(par, 8 + i, ktm[par], ('ktm', par))

            def q_stage2(i):
                par = i % 2
                tbk = 2 + par
                for hl in range(4):
                    o = bank_bf(tbk)[:, hl * 128:(hl + 1) * 128]
                    t.op('pe', lambda e, o=o, par=par, hl=hl: e.transpose(out=o, in_=ktm[par][:, hl * 128:(hl + 1) * 128], identity=ident),
                         reads=[('ktm', par), 'cb'], writes=[('ps', tbk)], sig=(hl == 3))
                srcT = bank_bf(tbk)[:, 0:512].rearrange("p (a n) -> p a n", a=4)
                t.op('dve' if par else 'act',
                     (lambda e, srcT=srcT, i=i: e.tensor_copy(out=QT[:, :, i * 128:(i + 1) * 128], in_=srcT)) if par else
                     (lambda e, srcT=srcT, i=i: e.activation(out=QT[:, :, i * 128:(i + 1) * 128], in_=srcT, func=AF.Copy)),
                     reads=[('ps', tbk)], writes=[('QT', i)])
            for i in range(9):
                if i < 8:
                    q_stage1(i)
                if i > 0:
                    q_stage2(i - 1)
            if stop_after == 'C1':
                return finish()
            zb_ring = wring[:, 2 * 4096:3 * 4096].rearrange("p (a n) -> p a n", a=16)

            def za_ap(dc):
                return ozT4[:, dc // 2, 2 + dc % 2, :]

            def zb_ap(dc):
                return ozT4[:, dc // 2, dc % 2, :] if hg == 0 else zb_ring[:, dc, :]
            for i in range(8):
                tb = 8 + i
                par = i % 2
                for half, getw, wres in ((0, za_ap, za_res), (1, zb_ap, zb_res)):
                    for dc in range(16):
                        wap = getw(dc)
                        t.op('pe', lambda e, dc=dc, tb=tb, par=par, half=half, wap=wap: e.matmul(
                            bank(par, 256, 256 * half), lhsT=hT[:, dc, tb * 128:(tb + 1) * 128], rhs=wap, start=(dc == 0), stop=(dc == 15)),
                            reads=wres + [('hT', tb, 0), ('hT', tb, 1)], writes=[('ps', par)], sig=(dc == 15))
                t.op('act', lambda e, par=par: e.activation(out=ztm[par], in_=bank(par), func=AF.Silu), reads=[('ps', par)], writes=[('ztm', par)])
                t.op('dve', lambda e, par=par, i=i: e.tensor_tensor(
                    out=zsb[:, i, :].rearrange("p (h e) -> p h e", h=4), in0=ztm[par].rearrange("p (h e) -> p h e", h=4),
                    in1=sublnw.unsqueeze(1).to_broadcast([128, 4, 128]), op=ALU.mult),
                    reads=[('ztm', par), 'sublnw'], writes=[('zsb', i)])
            if debug and hg == 0:
                t.dma('sp', dbg["d_KT"], KT.rearrange("p a n -> p (a n)"), reads=[('KT', tb) for tb in range(16)], key=('dbg', 5))
                t.dma('sp', dbg["d_Q0"], QT.rearrange("p a n -> p (a n)"), reads=[('QT', i) for i in range(8)], key=('dbg', 6))
                t.dma('sp', dbg["d_vaug"], vaug.rearrange("p m h e -> p (m h e)"), reads=[('vaug', tb) for tb in range(16)] + ['vones', 'vpad'], key=('dbg', 7))
                t.dma('sp', dbg["d_zsb"], zsb.rearrange("p a n -> p (a n)"), reads=[('zsb', i) for i in range(8)], key=('dbg', 8))

            if stop_after == 'C2':
                return finish()
            def acc_ap(ii, c, n=129, off=0):
                return bank(4 + ii, n, c * 129 + off), ('ps', 4 + ii)

            steps = []
            for hl in range(4):
                for g in range(2):
                    for m in range(8 + 4 * g + 4):
                        steps.append((hl, g, m))

            def emit_qk_exp(si):
                hl, g, m = steps[si]
                s0 = 0 if m < 8 + 4 * g else (m - 8 - 4 * g) * 128
                n = 512 - s0
                et = ET[si % 3]
                eres = ('ET', si % 3)
                pb = 2 * (si % 2)
                qres = [('QT', i) for i in range(4 * g, 4 * g + 4)]
                qres1 = qres
                diag_step = (m >= 8 + 4 * g)
                if WARM_JUNK:
                    for jb in range(WARM_JUNK):
                        t.op('pe', lambda e, jb=jb: e.matmul(bank(pb + jb % 2), lhsT=KT[:, hl, m * 128:(m + 1) * 128],
                                                            rhs=QT[:, hl, g * 512:g * 512 + 512], start=True, stop=True),
                             reads=[('KT', m)] + qres, writes=[('ps', pb + jb % 2)], sig=False)
                t.op('pe', lambda e: e.matmul(bank(pb, n, s0), lhsT=KT[0:64, hl, m * 128:(m + 1) * 128],
                                              rhs=QT[0:64, hl, g * 512 + s0:g * 512 + 512], start=True, stop=not diag_step),
                     reads=[('KT', m)] + qres, writes=[('ps', pb)], sig=False)
                t.op('pe', lambda e: e.matmul(bank(pb + 1, n, s0), lhsT=KT[64:128, hl, m * 128:(m + 1) * 128],
                                              rhs=QT[64:128, hl, g * 512 + s0:g * 512 + 512], start=True, stop=not diag_step),
                     reads=[('KT', m)] + qres1, writes=[('ps', pb + 1)], sig=not diag_step)
                if diag_step:
                    t.op('pe', lambda e: e.matmul(bank(pb, 128, s0), lhsT=ident, rhs=tri, start=False, stop=True),
                         reads=['cb'], writes=[('ps', pb)], sig=False)
                    t.op('pe', lambda e: e.matmul(bank(pb + 1, 128, s0), lhsT=ident, rhs=tri, start=False, stop=True),
                         reads=['cb'], writes=[('ps', pb + 1)], sig=True)
                src = ps[:, 512 * pb:512 * pb + 1024].rearrange("p (c n) -> p c n", c=2)[:, :, s0:512]
                t.op('act', lambda e: e.activation(out=et[:, :, s0:512], in_=src, func=AF.Exp, scale=0.125),
                     reads=[('ps', pb), ('ps', pb + 1)], writes=[eres])

            def emit_pv(si):
                global_fk = fkc[0]
                hl, g, m = steps[si]
                h = 4 * hg + hl
                s0 = 0 if m < 8 + 4 * g else (m - 8 - 4 * g) * 128
                et = ET[si % 3]
                eres = ('ET', si % 3)
                for ii in range(s0 // 128, 4):
                    last = (m == 8 + 4 * g + ii)
                    for c in range(2):
                        oap, ores = acc_ap(ii, c)
                        t.op('pe', lambda e, oap=oap, c=c, ii=ii, last=last: e.matmul(
                            oap, lhsT=et[:, c, ii * 128:(ii + 1) * 128], rhs=vaug[:, m, hl, 0:129],
                            start=(m == 0 and c == 0), stop=(last and c == 1)),
                            reads=[eres, ('vaug', m), 'vones'], writes=[ores], sig=(last and c == 1))
                    if last:
                        finalize(h, hl, 4 * g + ii, ii)

            def finalize(h, hl, i, ii):
                fk = fkc[0]
                fkc[0] += 1
                st = fin[:, fk, :]
                fres = ('fin', fk)
                fp = fk % 4
                o0, r0 = acc_ap(ii, 0, 128)
                o1, r1 = acc_ap(ii, 1, 128)
                s0a, _ = acc_ap(ii, 0, 1, 128)
                s1a, _ = acc_ap(ii, 1, 1, 128)
                t.op('dve', lambda e: e.reciprocal(out=st[:, 0:1], in_=s0a), reads=[r0, 'small'], writes=[fres + (0,)])
                t.op('dve', lambda e: e.reciprocal(out=st[:, 1:2], in_=s1a), reads=[r1, 'small'], writes=[fres + (1,)])
                t.op('dve', lambda e: e.tensor_tensor(out=st[:, 1:2], in0=st[:, 1:2], in1=neglam, op=ALU.mult),
                     reads=[fres + (1,), 'neglam'], writes=[fres + (1,)])
                t.op('dve', lambda e: e.tensor_scalar(out=t0s[fp], in0=o0, scalar1=st[:, 0:1], scalar2=None, op0=ALU.mult),
                     reads=[r0, fres + (0,)], writes=[('t0s', fp)])
                t.op('dve', lambda e: e.scalar_tensor_tensor(out=osb[fp], in0=o1, scalar=st[:, 1:2], in1=t0s[fp], op0=ALU.mult, op1=ALU.add),
                     reads=[r1, fres + (1,), ('t0s', fp)], writes=[('osb', fp)])
                def stage1b():
                    t.op('dve', lambda e: e.scalar_tensor_tensor(out=t0s[fp], in0=osb[fp], scalar=1.0, in1=osb[fp], op0=ALU.mult, op1=ALU.mult,
                                                                 accum_out=st[:, 2:3]),
                         reads=[('osb', fp), 'small'], writes=[('t0s', fp), fres + (2,)])
                pending_b.append((cur[0], stage1b))

                def stage2():
                    t.op('act', lambda e: e.activation(out=st[:, 3:4], in_=st[:, 2:3], func=AF.Ln, scale=1.0 / 128, bias=eps5),
                         reads=[fres + (2,), 'eps5'], writes=[fres + (3,)])
                    t.op('act', lambda e: e.activation(out=st[:, 4:5], in_=st[:, 3:4], func=AF.Exp, scale=-0.5),
                         reads=[fres + (3,)], writes=[fres + (4,)])
                    t.op('dve', lambda e: e.scalar_tensor_tensor(
                        out=ozT[:, i, h * 128:(h + 1) * 128], in0=osb[fp], scalar=st[:, 4:5], in1=zsb[:, i, hl * 128:(hl + 1) * 128],
                        op0=ALU.mult, op1=ALU.mult),
                        reads=[('osb', fp), fres + (4,), ('zsb', i)], writes=[('oz', i, h)])
                pending.append((cur[0], stage2))

            pending = []
            pending_b = []
            cur = [0]

            def flush(upto):
                while pending_b and pending_b[0][0] <= upto + FIN_DELAY - 1:
                    pending_b.pop(0)[1]()
                while pending and pending[0][0] <= upto - 1:
                    pending.pop(0)[1]()

            emit_qk_exp(0)
            emit_qk_exp(1)
            for si in range(len(steps)):
                cur[0] = si
                if si + 2 < len(steps):
                    emit_qk_exp(si + 2)
                flush(si - FIN_DELAY)
                emit_pv(si)
            flush(len(steps))
        ozT_all = [('oz', a_, b_) for a_ in range(8) for b_ in range(8)]

        if stop_after == 'D':
            return finish()
        def f_load(pi):
            ua = (pi % 2) * 2
            rga = load_w(ua * 4096, 4096, w_in_v[:, :, C_GA + pi * 256:C_GA + pi * 256 + 256], ('Wk', ua))
            return ua, rga

        def f_load2(pi):
            ua = (pi % 2) * 2
            rgc = load_w((ua + 1) * 4096, 4096, w_in_v[:, :, C_GC + pi * 256:C_GC + pi * 256 + 256], ('Wk', ua + 1))
            return rgc
        pend = f_load(0)
        pend2 = f_load2(0)
        t.barrier()
        A.reset()
        mT = A.bf(16 * 1024).rearrange("p (a n) -> p a n", a=16)
        slots = [(a_, a_) for a_ in range(8)]
        for a_ in range(8):
            for b_ in range(a_ + 1, 8):
                slots += [(a_, b_), (b_, a_)]
        for r0 in range(0, 64, 8):
            bk = (r0 // 8) % 4
            grp = slots[r0:r0 + 8]
            for k, (a_, b_) in enumerate(grp):
                o = bank_bf(bk)[:, k * 128:(k + 1) * 128]
                t.op('pe', lambda e, o=o, a_=a_, b_=b_: e.transpose(out=o, in_=ozT[:, a_, b_ * 128:(b_ + 1) * 128], identity=ident),
                     reads=[('oz', a_, b_), 'cb'], writes=[('ps', bk)], sig=(k == 7))
            for k, (a_, b_) in enumerate(grp):
                o = bank_bf(bk)[:, k * 128:(k + 1) * 128]
                if bk % 2 == 0:
                    t.op('act', lambda e, o=o, a_=a_, b_=b_: e.activation(out=ozT[:, b_, a_ * 128:(a_ + 1) * 128], in_=o, func=AF.Copy),
                         reads=[('ps', bk)], writes=[('oz', b_, a_)])
                else:
                    t.op('dve', lambda e, o=o, a_=a_, b_=b_: e.tensor_copy(out=ozT[:, b_, a_ * 128:(a_ + 1) * 128], in_=o),
                         reads=[('ps', bk)], writes=[('oz', b_, a_)])
        if debug:
            t.dma('sp', dbg["d_ozT"], ozT_t[:, :], reads=ozT_all, key=('dbg', 9))
        wop = [A.bf(4096), A.bf(4096)]
        sga = A.f32(512)
        sgc = A.f32(512)
        f1 = A.f32(512)
        f2 = A.f32(512)
        wvs = A.f32(D)
        t.dma('sp', wvs, wvec_d[:, D:2 * D], writes=['wv2'], key='wv2')
        junkG = A.bf(D)

        def wop_load(pi):
            sl = wop[pi % 2]
            dst = sl.rearrange("p (w a n) -> p w a n", w=2, a=8)
            k = ('wop', pi % 2)
            t.dma('pool', dst[:, 0, :, :], w_o_v[:, :, pi * 256:(pi + 1) * 256], writes=[('wop', pi % 2, 0)], key=k)
            tok = t.dma('pool', dst[:, 1, :, :], w_pw2_v[:, :, pi * 256:(pi + 1) * 256], writes=[('wop', pi % 2, 1)], key=k)
            t.lastw[('wop', pi % 2, 0)] = tok
        wop_load(0)
        def woutB_load(q4):
            dst = wring[:, 4096 * q4:4096 * (q4 + 1)].rearrange("p (a n) -> p a n", a=4)
            t.dma('pool', dst, w_out_v[:, 4 * q4:4 * q4 + 4, 1024:2048],
                  writes=[('W', q) for q in range(4 * q4, 4 * q4 + 4)], key=('woutB', q4))

        for pi in range(8):
            ua, rga = pend
            rgc = pend2
            if pi == 7:
                woutB_load(0)
                woutB_load(1)
            wga = wring[:, ua * 4096:(ua + 1) * 4096].rearrange("p (a n) -> p a n", a=16)
            wgc = wring[:, (ua + 1) * 4096:(ua + 2) * 4096].rearrange("p (a n) -> p a n", a=16)
            wsl = wop[pi % 2].rearrange("p (w a n) -> p w a n", w=2, a=8)
            if pi + 1 < 8:
                pend = f_load(pi + 1)
                pend2 = f_load2(pi + 1)
                wop_load(pi + 1)
            for ci in range(2):
                nch = pi * 2 + ci
                for tg in range(2):
                    par = it % 2
                    it += 1
                    b0 = 4 * par
                    tc0 = 1024 + tg * 512
                    for dc in range(16):
                        t.op('pe', lambda e, dc=dc, b0=b0, wga=wga, ci=ci, tc0=tc0: e.matmul(
                            bank(b0), lhsT=wga[:, dc, ci * 128:(ci + 1) * 128], rhs=hT[:, dc, tc0:tc0 + 512], start=(dc == 0), stop=(dc == 15)),
                            reads=rga + hT_own, writes=[('ps', b0)], sig=(dc == 15))
                    for dc in range(16):
                        t.op('pe', lambda e, dc=dc, b0=b0, wgc=wgc, ci=ci, tc0=tc0: e.matmul(
                            bank(b0 + 1), lhsT=wgc[:, dc, ci * 128:(ci + 1) * 128], rhs=hT[:, dc, tc0:tc0 + 512], start=(dc == 0), stop=(dc == 15)),
                            reads=rgc + hT_own, writes=[('ps', b0 + 1)], sig=(dc == 15))
                    for fc in range(8):
                        t.op('pe', lambda e, fc=fc, b0=b0, wsl=wsl, ci=ci, tg=tg: e.matmul(
                            bank(b0 + 2), lhsT=wsl[:, 0, fc, ci * 128:(ci + 1) * 128], rhs=ozT[:, fc, tg * 512:(tg + 1) * 512], start=(fc == 0), stop=(fc == 7)),
                            reads=[('wop', pi % 2, 0)] + ozT_all, writes=[('ps', b0 + 2)], sig=(fc == 7))
                    for fc in range(8):
                        t.op('pe', lambda e, fc=fc, b0=b0, wsl=wsl, ci=ci, tg=tg: e.matmul(
                            bank(b0 + 3), lhsT=wsl[:, 1, fc, ci * 128:(ci + 1) * 128], rhs=u2T[:, fc, tg * 512:(tg + 1) * 512], start=(fc == 0), stop=(fc == 7)),
                            reads=[('wop', pi % 2, 1)] + [('u2T', fc, tg) for fc in range(8)], writes=[('ps', b0 + 3)], sig=(fc == 7))
                    t.op('act', lambda e, b0=b0: e.activation(out=sga, in_=bank(b0), func=AF.Sigmoid), reads=[('ps', b0)], writes=['sga'])
                    t.op('act', lambda e, b0=b0: e.activation(out=sgc, in_=bank(b0 + 1), func=AF.Sigmoid), reads=[('ps', b0 + 1)], writes=['sgc'])
                    t.op('dve', lambda e, b0=b0: e.tensor_tensor(out=f1, in0=bank(b0 + 2), in1=sga, op=ALU.mult), reads=[('ps', b0 + 2), 'sga'], writes=['f1'])
                    t.op('dve', lambda e, b0=b0, nch=nch: e.scalar_tensor_tensor(out=f2, in0=bank(b0 + 3), scalar=bpw2[:, nch:nch + 1], in1=sgc,
                                                                               op0=ALU.add, op1=ALU.mult),
                         reads=[('ps', b0 + 3), 'sgc', 'cf'], writes=['f2'])
                    t.op('dve', lambda e, nch=nch, tg=tg: e.tensor_tensor(out=mT[:, nch, tg * 512:(tg + 1) * 512], in0=f1, in1=f2, op=ALU.add),
                         reads=['f1', 'f2'], writes=[('mT', nch, tg)])
        if debug:
            t.dma('sp', dbg["d_mT"], mT.rearrange("p a n -> p (a n)"), reads=[('mT', n_, g_) for n_ in range(16) for g_ in range(2)], key=('dbg', 10))

        if stop_after == 'F':
            return finish()
        u2T_all = [('u2T', cc, th) for cc in range(8) for th in range(2)]
        woutB = wring.rearrange("p (a n) -> p a n", a=16)
        woutB_load(2)
        woutB_load(3)
        xr = [ozT_t[:, 0:4096].bitcast(F32), ozT_t[:, 4096:8192].bitcast(F32)]
        ost = [u2T_t[:, 0:4096].bitcast(F32), u2T_t[:, 4096:8192].bitcast(F32)]
        for i in range(8):
            par = i % 2
            b0 = 4 * par
            t.dma('sp', xr[par], x_all[1024 + i * 128:1024 + (i + 1) * 128, :],
                  writes=[('xr', par)] + (ozT_all if i < 2 else []), key=('xr', par))
            for npn in range(4):
                c0 = (npn % 2) * 512
                for dc in range(16):
                    wsrc = woutA if npn < 2 else woutB
                    wres = [('woutA', dc // 4)] if npn < 2 else [('W', q) for q in range(dc, dc + 1)]
                    t.op('pe', lambda e, dc=dc, b0=b0, npn=npn, wsrc=wsrc, c0=c0, i=i: e.matmul(
                        bank(b0 + npn), lhsT=mT[:, dc, i * 128:(i + 1) * 128], rhs=wsrc[:, dc, c0:c0 + 512], start=(dc == 0), stop=(dc == 15)),
                        reads=wres + [('mT', dc, i // 4)], writes=[('ps', b0 + npn)], sig=(dc == 15))
            pall = ps[:, 512 * b0:512 * b0 + 2048]
            pres = [('ps', b0 + k) for k in range(4)]
            ores = [('ostA', par), ('ostB', par)]
            t.op('act', lambda e, pall=pall, i=i: e.activation(out=junkG, in_=pall, func=AF.Square, accum_out=ssq2[:, i:i + 1]),
                 reads=pres + ['small'], writes=['junkG', ('ssq2', i)])
            t.op('act', lambda e, i=i: e.activation(out=sq2[:, i:i + 1], in_=ssq2[:, i:i + 1], func=AF.Sqrt, scale=1.0 / D, bias=eps6),
                 reads=[('ssq2', i), 'eps6'], writes=[('sq2', i)])
            t.op('dve', lambda e, i=i: e.reciprocal(out=rstd2[:, i:i + 1], in_=sq2[:, i:i + 1]), reads=[('sq2', i)], writes=[('rstd2', i)])
            for hh in range(2):
                c0 = 1024 * hh
                ph = [('ps', b0 + 2 * hh), ('ps', b0 + 2 * hh + 1)]
                t.op('dve', lambda e, pall=pall, i=i, par=par, c0=c0: e.scalar_tensor_tensor(
                    out=ost[par][:, c0:c0 + 1024], in0=pall[:, c0:c0 + 1024], scalar=rstd2[:, i:i + 1], in1=wvs[:, c0:c0 + 1024],
                    op0=ALU.mult, op1=ALU.mult), reads=ph + [('rstd2', i), 'wv2'], writes=[ores[hh]] + (u2T_all if i < 2 else []))
                t.op('dve', lambda e, par=par, c0=c0: e.tensor_tensor(out=ost[par][:, c0:c0 + 1024], in0=ost[par][:, c0:c0 + 1024],
                                                                     in1=xr[par][:, c0:c0 + 1024], op=ALU.add),
                     reads=[('xr', par), ores[hh]], writes=[ores[hh]])
                t.dma('sp', out_d[i * 128:(i + 1) * 128, c0:c0 + 1024], ost[par][:, c0:c0 + 1024], reads=[ores[hh]], key=('out', par, hh))
        keys = [('out', p_, h_) for p_ in range(2) for h_ in range(2)] + [k for k in t.dcnt if isinstance(k, tuple) and k[0] == 'dbg']
        t.wait_all_dma('sp', keys)
        print("instr counts:", {e: len(t.prog[e]) for e in t.ENG}, "sems:", len(t.dsem) + 5)
        t.emit()
    return nc


def host_consts(inputs, half):
    cf = np.zeros((128, CF_N), np.float32)
    own0 = half * T
    pos = (own0 - T) + np.arange(2048, dtype=np.float64)
    inv = ROPE_THETA ** (-np.arange(0, 16, 2, dtype=np.float64) / 16)
    ang = (pos.astype(np.float32)[:, None] * inv.astype(np.float32)[None, :]).astype(np.float32).astype(np.float64)
    cos = np.cos(ang).astype(np.float32).reshape(16, 128, 8).transpose(1, 0, 2).reshape(128, 128)
    sin = np.sin(ang).astype(np.float32).reshape(16, 128, 8).transpose(1, 0, 2).reshape(128, 128)
    cf[:, CF_COS:CF_COS + 128] = cos
    cf[:, CF_SIN:CF_SIN + 128] = sin
    cf[:, CF_BGLU:CF_BGLU + 16] = inputs['b_glu'][0].reshape(16, 128).T
    cf[:, CF_BDW:CF_BDW + 8] = inputs['b_dw'][0].reshape(8, 128).T
    cf[:, CF_LNG:CF_LNG + 8] = inputs['ln_g'][0].reshape(8, 128).T
    cf[:, CF_LNB:CF_LNB + 8] = inputs['ln_b'][0].reshape(8, 128).T
    cf[:, CF_BPW2:CF_BPW2 + 16] = inputs['b_pw2'][0].reshape(16, 128).T
    cf[:, CF_WDW:CF_WDW + 248] = inputs['w_dw'][0].reshape(31, 8, 128).transpose(2, 1, 0).reshape(128, 248)
    lam = np.concatenate([inputs['lambda_q1'][0], inputs['lambda_k1'][0], inputs['lambda_q2'][0], inputs['lambda_k2'][0]])
    cf[:, CF_LAM:CF_LAM + 256] = lam[None, :]
    cf[:, CF_SUBLN:CF_SUBLN + 128] = inputs['subln_w'][0][None, :]
    cf[:, CF_HALO] = float(half)
    cb = np.zeros((128, CB_N), np.float32)
    cb[:, CB_ID:CB_ID + 128] = np.eye(128)
    kk = np.arange(128)[:, None]
    qq = np.arange(128)[None, :]
    cb[:, CB_TRI:CB_TRI + 128] = np.where(kk <= qq, 0.0, -30000.0)
    cb[:, CB_VONES:CB_VONES + 8] = float(half)
    cb[:, CB_VONES + 8:CB_VONES + 16] = 1.0
    return cf, cb.astype(ml_dtypes.bfloat16)


def make_in_maps(inputs):
    x = np.asarray(inputs['x'], np.float32)
    w_in = np.ascontiguousarray(np.asarray(inputs['w_in'], np.float32)[0])
    w_o = np.ascontiguousarray(np.asarray(inputs['w_o_attn'], np.float32)[0])
    w_pw2 = np.ascontiguousarray(np.asarray(inputs['w_pw2'], np.float32)[0])
    w_out = np.ascontiguousarray(np.asarray(inputs['w_out'], np.float32)[0])
    wvec = np.empty((128, 2 * D), np.float32)
    wvec[:, 0:D] = np.asarray(inputs['norm_pre_w'], np.float32)[0][None, :]
    wvec[:, D:] = np.asarray(inputs['norm_post_w'], np.float32)[0][None, :]
    npin = {k: np.asarray(v, np.float32) for k, v in inputs.items()}
    maps = []
    for core in range(8):
        b, half = core // 2, core % 2
        x_all = np.zeros((2048, D), np.float32)
        if half == 1:
            x_all[0:T] = x[b, 0:T]
        x_all[T:] = x[b, half * T:(half + 1) * T]
        cf, cb = host_consts(npin, half)
        maps.append({"x_all": x_all, "w_in": w_in, "w_o": w_o, "w_pw2": w_pw2, "w_out": w_out,
                     "cf": cf, "cb": cb, "wvec": wvec})
    return maps


def kernel(**inputs):
    nc = build()
    maps = make_in_maps(inputs)
    res = run_bass_kernel_spmd(nc, maps, core_ids=list(range(8)))
    out = np.empty((NB, S, D), np.float32)
    for core in range(8):
        b, half = core // 2, core % 2
        out[b, half * T:(half + 1) * T] = np.asarray(res.results[core]["out"], np.float32)
    return out
```
